# Optimizing a Trainium2 kernel written in Bass

```python
import jax, jax.numpy as jnp
from jax import lax
import numpy as np

D_MODEL = 1024
BATCH = 1
SEQ = 16384
DEPTH = 1
DEC_BATCH = 16
DEC_SEQ = 2048
PAST_LEN = 128

HEAD_DIM = 64
M_HEADS = 8
M_WIDTH = M_HEADS * HEAD_DIM
A_HEADS = 8
A_KV_HEADS = 2
A_WIDTH = A_HEADS * HEAD_DIM
KV_WIDTH = A_KV_HEADS * HEAD_DIM
MIX_WIDTH = M_WIDTH + A_WIDTH
N_GATES = 4 * M_HEADS
IN_COLS = 4 * M_WIDTH + N_GATES + A_WIDTH + 2 * KV_WIDTH
D_FF = 2816
CONV_W = 3
GRID_W = 64
CHUNK = 64
Q_BLOCK = 128
ROPE_THETA = 10000.0
EPS = 1e-6

kernel_name = "hybrid_mlstm_gqa_convffn_encoder"


def rmsnorm(x, w):
    xf = x.astype(jnp.float32)
    y = xf * lax.rsqrt(jnp.mean(xf * xf, axis=-1, keepdims=True) + EPS)
    return (y * w.astype(jnp.float32)).astype(x.dtype)


def mlstm_scan(q, k, v, ig, fg):
    B, H, T, d = q.shape
    nc = T // CHUNK
    logf = jax.nn.log_sigmoid(fg)

    def to_chunks(a):
        a = a.reshape((B, H, nc, CHUNK) + a.shape[3:])
        return jnp.moveaxis(a, 2, 0)

    qc, kc, vc, ic, fc = (to_chunks(a) for a in (q, k, v, ig, logf))
    causal = jnp.tril(jnp.ones((CHUNK, CHUNK), dtype=bool))

    def step(carry, inp):
        C, n, m = carry
        qb, kb, vb, ib, fb = inp
        b = jnp.cumsum(fb, axis=-1)
        Dm = b[..., :, None] - b[..., None, :] + ib[..., None, :]
        Dm = jnp.where(causal, Dm, -jnp.inf)
        inter = b + m[..., None]
        m_t = jnp.maximum(inter, jnp.max(Dm, axis=-1))
        Dw = jnp.exp(Dm - m_t[..., None])
        iw = jnp.exp(inter - m_t)
        s = jnp.einsum('bhld,bhsd->bhls', qb, kb) * Dw
        num = iw[..., None] * jnp.einsum('bhld,bhde->bhle', qb, C) + jnp.einsum('bhls,bhse->bhle', s, vb)
        den = iw * jnp.einsum('bhld,bhd->bhl', qb, n) + jnp.sum(s, axis=-1)
        h = num / jnp.maximum(jnp.abs(den), jnp.exp(-m_t))[..., None]
        bL = b[..., -1]
        wend = bL[..., None] - b + ib
        m_new = jnp.maximum(bL + m, jnp.max(wend, axis=-1))
        dec = jnp.exp(bL + m - m_new)
        we = jnp.exp(wend - m_new[..., None])
        C_new = dec[..., None, None] * C + jnp.einsum('bhs,bhsd,bhse->bhde', we, kb, vb)
        n_new = dec[..., None] * n + jnp.einsum('bhs,bhsd->bhd', we, kb)
        return (C_new, n_new, m_new), h

    init = (jnp.zeros((B, H, d, d), jnp.float32), jnp.zeros((B, H, d), jnp.float32),
            jnp.zeros((B, H), jnp.float32))
    _, hs = lax.scan(step, init, (qc, kc, vc, ic, fc))
    return jnp.moveaxis(hs, 0, 2).reshape(B, H, T, d)


def mlstm_mixer(mq, mk, mv, mo, gates, b_gates, mh_norm_w):
    B, T, _ = mq.shape

    def heads(a):
        return a.reshape(B, T, M_HEADS, HEAD_DIM).transpose(0, 2, 1, 3).astype(jnp.float32)

    q = heads(mq)
    k = heads(mk) * (HEAD_DIM ** -0.5)
    v = heads(mv)
    g = (gates.astype(jnp.float32) + b_gates.astype(jnp.float32)).transpose(0, 2, 1)
    i_f, i_b, f_f, f_b = jnp.split(g, 4, axis=1)
    h_f = mlstm_scan(q, k, v, i_f, f_f)
    fl = lambda a: jnp.flip(a, axis=2)
    h_b = fl(mlstm_scan(fl(q), fl(k), fl(v), fl(i_b), fl(f_b)))
    h = h_f + h_b
    h = h * lax.rsqrt(jnp.mean(h * h, axis=-1, keepdims=True) + EPS)
    h = h * mh_norm_w.astype(jnp.float32).reshape(M_HEADS, 1, HEAD_DIM)
    h = h.transpose(0, 2, 1, 3).reshape(B, T, M_WIDTH)
    return (jax.nn.sigmoid(mo.astype(jnp.float32)) * h).astype(mq.dtype)


def axial_rope_tables(T):
    rows = T // GRID_W
    row = jnp.repeat(jnp.arange(rows, dtype=jnp.float32), GRID_W)
    col = jnp.tile(jnp.arange(GRID_W, dtype=jnp.float32), rows)
    nf = HEAD_DIM // 4
    inv = ROPE_THETA ** (-jnp.arange(nf, dtype=jnp.float32) / nf)
    ar = row[:, None] * inv
    ac = col[:, None] * inv
    ang = jnp.concatenate([ar, ar, ac, ac], axis=-1)
    return jnp.cos(ang), jnp.sin(ang)


def apply_rope(x, cos, sin):
    x1r, x2r, x1c, x2c = jnp.split(x, 4, axis=-1)
    rot = jnp.concatenate([-x2r, x1r, -x2c, x1c], axis=-1)
    return (x.astype(jnp.float32) * cos + rot.astype(jnp.float32) * sin).astype(x.dtype)


def attention_mixer(aq, ak, av, q_norm_w, k_norm_w):
    B, T, _ = aq.shape
    G = A_HEADS // A_KV_HEADS
    q = rmsnorm(aq.reshape(B, T, A_KV_HEADS, G, HEAD_DIM), q_norm_w)
    k = rmsnorm(ak.reshape(B, T, A_KV_HEADS, HEAD_DIM), k_norm_w)
    v = av.reshape(B, T, A_KV_HEADS, HEAD_DIM)
    cos, sin = axial_rope_tables(T)
    q = apply_rope(q, cos[None, :, None, None], sin[None, :, None, None])
    k = apply_rope(k, cos[None, :, None], sin[None, :, None])
    nb = T // Q_BLOCK
    qb = q.reshape(B, nb, Q_BLOCK, A_KV_HEADS, G, HEAD_DIM).transpose(1, 0, 2, 3, 4, 5)
    scale = HEAD_DIM ** -0.5

    def block(qi):
        s = jnp.einsum('bqhgd,bkhd->bhgqk', qi, k).astype(jnp.float32) * scale
        p = jax.nn.softmax(s, axis=-1)
        return jnp.einsum('bhgqk,bkhd->bqhgd', p.astype(v.dtype), v)

    o = lax.map(block, qb)
    return o.transpose(1, 0, 2, 3, 4, 5).reshape(B, T, A_WIDTH)


def centred_dwconv(u, w, b):
    up = jnp.pad(u, ((0, 0), (1, 1), (0, 0)))
    return up[:, :-2] * w[0] + up[:, 1:-1] * w[1] + up[:, 2:] * w[2] + b


def encoder_layer(x, w_in, b_gates, mh_norm_w, q_norm_w, k_norm_w, w_out, norm1_w, norm2_w,
                  w_up, conv_w, conv_b, w_down):
    h = rmsnorm(x, norm1_w)
    p = h @ w_in
    cuts = np.cumsum([M_WIDTH, M_WIDTH, M_WIDTH, M_WIDTH, N_GATES, A_WIDTH, KV_WIDTH]).tolist()
    mq, mk, mv, mo, gates, aq, ak, av = jnp.split(p, cuts, axis=-1)
    m_out = mlstm_mixer(mq, mk, mv, mo, gates, b_gates, mh_norm_w)
    a_out = attention_mixer(aq, ak, av, q_norm_w, k_norm_w)
    x = x + jnp.concatenate([m_out, a_out], axis=-1) @ w_out
    h = rmsnorm(x, norm2_w)
    u = centred_dwconv(h @ w_up, conv_w, conv_b)
    a, g = jnp.split(u, 2, axis=-1)
    return x + (jax.nn.silu(g) * a) @ w_down


def run_trunk(x, w_in, b_gates, mh_norm_w, q_norm_w, k_norm_w, w_out, norm1_w, norm2_w,
              w_up, conv_w, conv_b, w_down, final_norm_w):
    for l in range(DEPTH):
        x = encoder_layer(x, w_in[l], b_gates[l], mh_norm_w[l], q_norm_w[l], k_norm_w[l], w_out[l],
                          norm1_w[l], norm2_w[l], w_up[l], conv_w[l], conv_b[l], w_down[l])
    return rmsnorm(x, final_norm_w)


def setup_inputs(seed: int = 0) -> dict:
    key = jax.random.key(seed)
    ks = jax.random.split(key, 20)
    nrm = jax.random.normal
    f32 = jnp.float32
    forget_bias = jnp.tile(jnp.linspace(3.0, 6.0, M_HEADS, dtype=f32), 2)
    b_gates = jnp.concatenate([
        0.1 * nrm(ks[0], (DEPTH, 2 * M_HEADS), f32),
        forget_bias[None] + 0.1 * nrm(ks[1], (DEPTH, 2 * M_HEADS), f32),
    ], axis=-1)
    conv_center = jnp.array([0.0, 1.0, 0.0], f32).reshape(1, 3, 1)
    return {
        "x_prompt": nrm(ks[2], (BATCH, SEQ, D_MODEL), f32),
        "x_sample": nrm(ks[3], (DEC_BATCH, DEC_SEQ, D_MODEL), f32),
        "w_in": nrm(ks[4], (DEPTH, D_MODEL, IN_COLS), f32) * D_MODEL ** -0.5,
        "b_gates": b_gates,
        "mh_norm_w": 1.0 + 0.05 * nrm(ks[5], (DEPTH, M_WIDTH), f32),
        "q_norm_w": 1.0 + 0.05 * nrm(ks[6], (DEPTH, HEAD_DIM), f32),
        "k_norm_w": 1.0 + 0.05 * nrm(ks[7], (DEPTH, HEAD_DIM), f32),
        "w_out": nrm(ks[8], (DEPTH, MIX_WIDTH, D_MODEL), f32) * MIX_WIDTH ** -0.5,
        "norm1_w": 1.0 + 0.05 * nrm(ks[9], (DEPTH, D_MODEL), f32),
        "norm2_w": 1.0 + 0.05 * nrm(ks[10], (DEPTH, D_MODEL), f32),
        "w_up": nrm(ks[11], (DEPTH, D_MODEL, 2 * D_FF), f32) * D_MODEL ** -0.5,
        "conv_w": conv_center + 0.3 * nrm(ks[12], (DEPTH, CONV_W, 2 * D_FF), f32),
        "conv_b": 0.01 * nrm(ks[13], (DEPTH, 2 * D_FF), f32),
        "w_down": nrm(ks[14], (DEPTH, D_FF, D_MODEL), f32) * D_FF ** -0.5,
        "final_norm_w": 1.0 + 0.05 * nrm(ks[15], (D_MODEL,), f32),
    }


def reference(x_prompt, x_sample, w_in, b_gates, mh_norm_w, q_norm_w, k_norm_w, w_out, norm1_w, norm2_w,
              w_up, conv_w, conv_b, w_down, final_norm_w):
    y_prompt = run_trunk(x_prompt, w_in, b_gates, mh_norm_w, q_norm_w, k_norm_w, w_out, norm1_w, norm2_w,
                         w_up, conv_w, conv_b, w_down, final_norm_w)
    y_sample = run_trunk(x_sample, w_in, b_gates, mh_norm_w, q_norm_w, k_norm_w, w_out, norm1_w, norm2_w,
                         w_up, conv_w, conv_b, w_down, final_norm_w)
    return (y_prompt, y_sample)
```

```python
import os
import numpy as np
from contextlib import ExitStack
import concourse.bass as bass
import concourse.mybir as mybir
from concourse.bass_utils import run_bass_kernel_spmd

F32 = mybir.dt.float32
BF16 = mybir.dt.bfloat16
AF = mybir.ActivationFunctionType
ALU = mybir.AluOpType
AX = mybir.AxisListType

D = 1024
HD = 64
MW = 512
AW = 512
KVW = 128
NG = 32
INC = 2848
DFF = 2816
NFC = 44
NPAIR = 22
EPS = 1e-6
C_MQ, C_MK, C_MV, C_MO, C_G, C_AQ, C_AK, C_AV = 0, 512, 1024, 1536, 2048, 2080, 2592, 2720


class Buf:
    __slots__ = ("name", "last_w", "readers", "dsem", "dcnt", "excl")

    def __init__(self, name, excl=False):
        self.name = name
        self.excl = excl
        self.last_w = None
        self.readers = []
        self.dsem = None
        self.dcnt = 0


class Sched:
    ENGS = ("pe", "act", "dve", "pool", "sp")

    def __init__(self, nc, stack):
        self.nc = nc
        self.stack = stack
        self.q = {e: [] for e in self.ENGS}
        self.sem = {}
        self.cnt = {}
        for e in ("pe", "act", "dve", "pool"):
            self.sem[e] = stack.enter_context(nc.semaphore("prog_" + e))
            self.cnt[e] = 0
        self.known = {e: {} for e in self.ENGS}
        self.dma_best = {}
        self.semobjs = {}
        self.ninst = 0
        self.free_dsems = []
        self.phase_owners = []
        self._rec = None
        self._rec_ret = None
        self.nflush = 0
        self.skip = set(int(x) for x in os.environ.get("KDEBUG_SKIP", "").split(",") if x)
        self.trace = bool(os.environ.get("KDEBUG_TRACE"))
        self.max_flush = int(os.environ.get("KDEBUG_STOP", "1000000"))
        self.max_ops = int(os.environ.get("KDEBUG_OPS", "100000000"))

    def _deps(self, eng, reads, writes):
        deps = {}

        def add(tok):
            if tok is None:
                return
            s, v = tok
            k = id(s)
            self.semobjs[k] = s
            if deps.get(k, 0) < v:
                deps[k] = v
        for b in reads:
            add(b.last_w)
            if b.excl:
                for r in b.readers:
                    add(r)
        for b in writes:
            add(b.last_w)
            for r in b.readers:
                add(r)
        out = []
        kn = self.known[eng]
        for k, v in deps.items():
            if eng == "pe" and self.semobjs[k] is self.sem["pe"]:
                continue
            if kn.get(k, 0) >= v:
                continue
            kn[k] = v
            out.append((self.semobjs[k], v))
        return out

    def _commit(self, tok, reads, writes):
        for b in writes:
            b.last_w = tok
            b.readers = []
        for b in reads:
            if b not in writes:
                if b.excl:
                    b.readers = [tok]
                    continue
                b.readers.append(tok)
                if len(b.readers) > 48:
                    best = {}
                    for s, v in b.readers:
                        if best.get(id(s), (None, 0))[1] < v:
                            best[id(s)] = (s, v)
                    b.readers = list(best.values())

    def _trace(self, eng):
        if self.trace:
            import traceback
            fr = traceback.extract_stack(limit=4)[0]
            print("OP", self.ninst, eng, fr.lineno, fr.line[:90], flush=True)

    def record(self, fn, *args):
        assert self._rec is None
        self._rec = []
        self._rec_ret = fn(*args)
        rec, self._rec = self._rec, None
        return rec

    def interleave(self, recs):
        recs = [list(r) for r in recs if r]
        idx = [0] * len(recs)
        live = True
        while live:
            live = False
            for k, r in enumerate(recs):
                if idx[k] < len(r):
                    kind, a = r[idx[k]]
                    idx[k] += 1
                    live = True
                    getattr(self, kind)(*a)

    def interleave_spread(self, main, side):
        n, m = len(main), len(side)
        j = 0
        for i, (kind, a) in enumerate(main):
            while j < m and j * n <= i * m:
                k2, a2 = side[j]
                j += 1
                getattr(self, k2)(*a2)
            getattr(self, kind)(*a)
        while j < m:
            k2, a2 = side[j]
            j += 1
            getattr(self, k2)(*a2)

    def dead(self):
        if self.ninst in self.skip:
            self.ninst += 1
            print("SKIPPED op", self.ninst - 1, flush=True)
            return True
        return self.nflush >= self.max_flush or self.ninst >= self.max_ops

    def op(self, eng, fn, reads=(), writes=()):
        if self._rec is not None:
            self._rec.append(("op", (eng, fn, list(reads), list(writes))))
            return None
        if self.dead():
            return None
        self._trace(eng)
        waits = self._deps(eng, reads, writes)
        self.cnt[eng] += 1
        tok = (self.sem[eng], self.cnt[eng])
        self.q[eng].append((waits, fn, tok, False))
        self._commit(tok, reads, writes)
        self.ninst += 1
        return tok

    def group(self, eng, fns, reads=(), writes=()):
        if self._rec is not None:
            self._rec.append(("group", (eng, fns, list(reads), list(writes))))
            return None
        if self.dead():
            return None
        self._trace(eng)
        waits = self._deps(eng, reads, writes)
        self.cnt[eng] += 1
        tok = (self.sem[eng], self.cnt[eng])
        n = len(fns)
        for i, fn in enumerate(fns):
            self.q[eng].append((waits if i == 0 else [], fn, tok if i == n - 1 else None, False))
        self._commit(tok, reads, writes)
        self.ninst += n
        return tok

    def dma(self, out_ap, in_ap, owner, reads=(), writes=(), q="sp"):
        if self._rec is not None:
            self._rec.append(("dma", (out_ap, in_ap, owner, list(reads), list(writes), q)))
            return None
        if self.dead():
            return None
        if owner.dsem is None:
            if self.free_dsems and q == "sp":
                owner.dsem, owner.dcnt = self.free_dsems.pop()
                self.phase_owners.append(owner)
            else:
                owner.dsem = self.stack.enter_context(self.nc.semaphore(_uniq("d_" + owner.name)))
                owner.dcnt = 0
                if q == "sp":
                    self.phase_owners.append(owner)
        self._trace("dma")
        waits = self._deps(q, reads, writes)
        owner.dcnt += 16
        tok = (owner.dsem, owner.dcnt)

        def fn(e, out_ap=out_ap, in_ap=in_ap):
            return e.dma_start(out=out_ap, in_=in_ap)
        self.q[q].append((waits, fn, tok, True))
        self._commit(tok, reads, writes)
        self.dma_best[id(owner.dsem)] = tok
        self.ninst += 1
        return tok

    def flush(self):
        nc = self.nc
        self.nflush += 1
        bar = [(self.sem[e], self.cnt[e]) for e in ("pe", "act", "dve", "pool") if self.cnt[e] > 0]
        bar += list(self.dma_best.values())
        qs = self.q
        self.q = {e: [] for e in self.ENGS}
        known = self.known

        def run(engname, e):
            for waits, fn, tok, isdma in qs[engname]:
                for s, v in waits:
                    e.wait_ge(s, v)
                ins = fn(e)
                if tok is not None:
                    ins.then_inc(tok[0], 16 if isdma else 1)
            for s, v in bar:
                if known[engname].get(id(s), 0) < v:
                    e.wait_ge(s, v)
                    known[engname][id(s)] = v

        with nc.allow_low_precision(reason="bf16 matmul operands by design"), nc.allow_non_contiguous_dma(reason="single halo columns"), nc.Block() as blk:
            @blk.tensor
            def _(e):
                run("pe", e)

            @blk.scalar
            def _(e):
                run("act", e)

            @blk.vector
            def _(e):
                run("dve", e)

            @blk.gpsimd
            def _(e):
                run("pool", e)

            @blk.sync
            def _(e):
                run("sp", e)
        for o in self.phase_owners:
            if o.dsem is not None:
                self.free_dsems.append((o.dsem, o.dcnt))
                o.dsem = None
        self.phase_owners = []


_UID = [0]
_HEADS = [int(x) for x in os.environ.get('KDEBUG_HEADS', '0,2,4,6,1,3,5,7').split(',')]


def _uniq(name):
    _UID[0] += 1
    return f"{name}_{_UID[0]}"


class Ring:
    def __init__(self, nc, st, name, shape, dt, n):
        self.items = []
        for i in range(n):
            t = st.enter_context(nc.sbuf_tensor(_uniq(f"{name}{i}"), shape, dt))
            self.items.append((t, Buf(f"{name}{i}")))
        self.i = 0

    def get(self):
        it = self.items[self.i % len(self.items)]
        self.i += 1
        return it


class Cfg:
    def __init__(self, ncores=8, seg=16, samp_tiles=16, nsamp=2, gt=4):
        self.NC = ncores
        self.SEG = seg
        self.ST = samp_tiles
        self.NSAMP = nsamp
        self.GT = gt
        self.NP = ncores * seg
        self.NCTX = self.NP - seg - 1
        self.segs = [("s%d" % i, samp_tiles, False) for i in range(nsamp)] + [("p", seg, True)]
        self.own_tiles = sum(m + (2 if h else 0) for _, m, h in self.segs)
        self.main_tiles = sum(m for _, m, h in self.segs)


def build_program(cfg):
    nc = bass.Bass("TRN2", target_bir_lowering=False)
    NOWN, NMAIN, NCTX = cfg.own_tiles, cfg.main_tiles, cfg.NCTX
    NTOK = NOWN * 128

    def din(name, shape, dt=F32):
        return nc.dram_tensor(name, shape, dt, kind="ExternalInput").ap()

    xs_d = din("xs", [NOWN, 128, D])
    xc_d = din("xc", [max(NCTX, 1), 128, D])
    cso_d = din("cs_own", [NOWN, 128, 128])
    csc_d = din("cs_ctx", [max(NCTX, 1), 128, 128])
    flc_d = din("fl_ctx", [128, max(NCTX, 1) * 3])
    flo_d = din("fl_own", [128, 4])
    w_in_d = din("w_in", [D, INC])
    w_out_d = din("w_out", [D, D])
    w_up_d = din("w_up", [D, 2 * DFF])
    w_down_d = din("w_down", [DFF, D])
    n1_d = din("n1pk", [128, 8])
    n2_d = din("n2pk", [128, 8])
    cw_d = din("convpk", [128, NFC * 4])
    bg_d = din("bg_bc", [128, NG])
    mhw_d = din("mhw_bc", [128, MW])
    wqk_d = din("wqk_bc", [128, 640])
    wfin_d = din("wfin_bc", [128, D])
    y_d = nc.dram_tensor("y", [NMAIN, 128, D], F32, kind="ExternalOutput").ap()
    mixT_d = nc.dram_tensor("mixT_s", [D, NTOK], BF16, kind="Internal").ap()
    x1_d = nc.dram_tensor("x1_s", [NMAIN, 128, D], F32, kind="Internal").ap()
    h2w = [m * 128 + 2 for _, m, _ in cfg.segs]
    h2T_d = [nc.dram_tensor(f"h2T_s{i}", [D, h2w[i]], BF16, kind="Internal").ap() for i in range(len(cfg.segs))]
    recd_d = nc.dram_tensor("recd_s", [4, 512], F32, kind="Internal").ap()
    b_recd = [Buf(f"recd{i}") for i in range(4)]
    w_in_s = nc.dram_tensor("w_in_bf", [D, INC], BF16, kind="Internal").ap()
    w_out_s = nc.dram_tensor("w_out_bf", [D, D], BF16, kind="Internal").ap()
    w_up_s = nc.dram_tensor("w_up_bf", [D, 2 * DFF], BF16, kind="Internal").ap()
    w_dn_s = nc.dram_tensor("w_dn_bf", [DFF, D], BF16, kind="Internal").ap()
    b_wins, b_wouts, b_wups, b_wdns = Buf("w_in_s"), Buf("w_out_s"), Buf("w_up_s"), Buf("w_dn_s")
    b_mixT = Buf("mixT_d")
    b_x1d = [Buf(f"x1d{i}") for i in range(NMAIN)]
    b_h2d = [Buf(f"h2d{i}") for i in range(len(cfg.segs))]
    b_y = Buf("y_d")

    with ExitStack() as gst:
        S = Sched(nc, gst)
        sb = lambda st, name, shape, dt: st.enter_context(nc.sbuf_tensor(_uniq(name), shape, dt))
        ps = lambda st, name, shape, dt: st.enter_context(nc.psum_tensor(_uniq(name), shape, dt))

        ident = sb(gst, "ident", [128, 128], BF16); b_ident = Buf("ident")
        idf = sb(gst, "idf", [128, 128], F32); b_idf = Buf("idf")
        trif = sb(gst, "trif", [128, 128], F32); b_trif = Buf("trif")
        onesf = sb(gst, "onesf", [128, 128], F32); b_onesf = Buf("onesf")
        maskf = sb(gst, "maskf", [128, 128], F32); b_maskf = Buf("maskf")
        maskb = sb(gst, "maskb", [128, 128], F32); b_maskb = Buf("maskb")
        epst = sb(gst, "epst", [128, 1], F32); b_eps = Buf("eps")
        onet = sb(gst, "onet", [128, 1], F32); b_one = Buf("one")
        bg = sb(gst, "bg", [128, NG], F32); b_bg = Buf("bg")
        mhw = sb(gst, "mhw", [128, MW], F32); b_mhw = Buf("mhw")
        wqk = sb(gst, "wqk", [128, 640], F32); b_wqk = Buf("wqk")
        wfin = sb(gst, "wfin", [128, D], F32); b_wfin = Buf("wfin")
        n1 = sb(gst, "n1", [128, 8], F32); b_n1 = Buf("n1")
        n2 = sb(gst, "n2", [128, 8], F32); b_n2 = Buf("n2")
        flo = sb(gst, "flo", [128, 4], F32); b_flo = Buf("flo")
        negc = sb(gst, "negc", [128, 1], F32); b_negc = Buf("negc")
        mx2 = sb(gst, "mx2", [128, 2], F32); b_mx2 = Buf("mx2")
        cinit = sb(gst, "cinit", [128, 2, 8, 65], F32); b_cinit = Buf("cinit")

        for t, b, v in ((idf, b_idf, 1.0), (trif, b_trif, 1.0), (onesf, b_onesf, 1.0), (epst, b_eps, EPS),
                        (onet, b_one, 1.0)):
            S.op("pool", lambda e, t=t, v=v: e.memset(t[:], v), writes=[b])
        S.op("pool", lambda e: e.affine_select(out=idf[:], in_=idf[:], pattern=[[-1, 128]], compare_op=ALU.is_equal,
                                               fill=0.0, base=0, channel_multiplier=1), reads=[b_idf], writes=[b_idf])
        S.op("pool", lambda e: e.affine_select(out=trif[:], in_=trif[:], pattern=[[1, 128]], compare_op=ALU.is_ge,
                                               fill=0.0, base=0, channel_multiplier=-1), reads=[b_trif], writes=[b_trif])
        S.op("dve", lambda e: e.tensor_copy(ident[:], idf[:]), reads=[b_idf], writes=[b_ident])
        S.op("dve", lambda e: e.tensor_copy(maskf[:], trif[:]), reads=[b_trif], writes=[b_maskf])
        S.op("pool", lambda e: e.memset(maskb[:], 1.0), writes=[b_maskb])
        S.op("pool", lambda e: e.affine_select(out=maskb[:], in_=maskb[:], pattern=[[-1, 128]], compare_op=ALU.is_ge,
                                               fill=0.0, base=0, channel_multiplier=1), reads=[b_maskb], writes=[b_maskb])
        for t, b, d in ((bg, b_bg, bg_d), (mhw, b_mhw, mhw_d), (wqk, b_wqk, wqk_d), (wfin, b_wfin, wfin_d),
                        (n1, b_n1, n1_d), (n2, b_n2, n2_d), (flo, b_flo, flo_d)):
            S.dma(t[:], d, b, writes=[b])
        S.op("dve", lambda e: e.tensor_scalar(out=wqk[:, 0:512], in0=wqk[:, 0:512], scalar1=0.125, scalar2=None,
                                              op0=ALU.mult), reads=[b_wqk], writes=[b_wqk])
        S.op("dve", lambda e: e.tensor_reduce(out=mx2[:, 0:1], in_=wqk[:, 0:64], axis=AX.X, op=ALU.max,
                                              apply_absolute_value=True), reads=[b_wqk], writes=[b_mx2])
        S.op("dve", lambda e: e.tensor_reduce(out=mx2[:, 1:2], in_=wqk[:, 512:576], axis=AX.X, op=ALU.max,
                                              apply_absolute_value=True), reads=[b_wqk, b_mx2], writes=[b_mx2])
        S.op("dve", lambda e: e.tensor_scalar(out=negc[:], in0=mx2[:, 0:1], scalar1=mx2[:, 1:2], scalar2=-64.0,
                                              op0=ALU.mult, op1=ALU.mult), reads=[b_mx2], writes=[b_negc])
        S.op("pool", lambda e: e.memset(cinit[:], 0.0), writes=[b_cinit])
        S.flush()

        def make_conv(stw, dmaq):
            stg_r = Ring(nc, stw, "cvs", [128, 2848], F32, 3)
            stb_r = Ring(nc, stw, "cvb", [128, 2848], BF16, 3)

            def conv_rows(src_ap, dst_ap, ncol, scale_ap, b_dst, eighth=None):
                stg, bstg = stg_r.get(); stb, bstb = stb_r.get()
                S.dma(stg[:, 0:ncol], src_ap, bstg, writes=[bstg], q=dmaq)
                if scale_ap is None:
                    S.op("dve", lambda e: e.tensor_copy(stb[:, 0:ncol], stg[:, 0:ncol]), reads=[bstg], writes=[bstb])
                else:
                    S.op("dve", lambda e: e.tensor_scalar(out=stb[:, 0:ncol], in0=stg[:, 0:ncol], scalar1=scale_ap, scalar2=None,
                                                          op0=ALU.mult), reads=[bstg, b_n1, b_n2], writes=[bstb])
                if eighth is not None:
                    a, b = eighth
                    S.op("dve", lambda e: e.tensor_scalar(out=stb[:, a:b], in0=stb[:, a:b], scalar1=0.125, scalar2=None,
                                                          op0=ALU.mult), reads=[bstb], writes=[bstb])
                S.dma(dst_ap, stb[:, 0:ncol], bstb, reads=[bstb], writes=[b_dst], q=dmaq)
            return conv_rows

        def conv_ffn_weights(conv_rows):
            for k in range(8):
                conv_rows(w_out_d[k * 128:(k + 1) * 128, :], w_out_s[k * 128:(k + 1) * 128, :], D, None, b_wouts)
            for k in range(8):
                for hh in range(2):
                    conv_rows(w_up_d[k * 128:(k + 1) * 128, hh * DFF:(hh + 1) * DFF], w_up_s[k * 128:(k + 1) * 128, hh * DFF:(hh + 1) * DFF],
                              DFF, n2[:, k:k + 1], b_wups)
            for j in range(NPAIR):
                conv_rows(w_down_d[j * 128:(j + 1) * 128, :], w_dn_s[j * 128:(j + 1) * 128, :], D, None, b_wdns)

        with ExitStack() as stw:
            conv_a = make_conv(stw, "sp")
            conv_b = make_conv(stw, "pool")
            for k in range(8):
                (conv_a if k % 2 == 0 else conv_b)(w_in_d[k * 128:(k + 1) * 128, :], w_in_s[k * 128:(k + 1) * 128, :], INC,
                                                   n1[:, k:k + 1], b_wins, eighth=(C_MK, C_MK + 512))
            S.flush()

        def load_w_bf(st, name, src_s, b_src, c0, c1):
            w = sb(st, name, [128, 8, c1 - c0], BF16)
            bw = Buf(name)
            S.dma(w[:], src_s[:, c0:c1].rearrange("(k p) c -> p k c", p=128), bw, reads=[b_src], writes=[bw])
            return w, bw

        def rsqrt_mean(dst, src, n, rd, wr):
            S.op("act", lambda e: e.activation(out=dst, in_=src, func=AF.Ln, scale=1.0 / n, bias=epst[:]),
                 reads=rd + [b_eps], writes=wr)
            S.op("act", lambda e: e.activation(out=dst, in_=dst, func=AF.Exp, scale=-0.5), reads=wr, writes=wr)

        class TileCtx:
            def __init__(self, st, pT, nb=2):
                self.xt = Ring(nc, st, "xt", [128, D], F32, nb)
                self.nb = nb
                self.sq = Ring(nc, st, "sqj", [128, D], BF16, 1)
                self.ss = Ring(nc, st, "ssq", [128, 2], F32, 2)
                self.hn = Ring(nc, st, "hn", [128, D], BF16, nb)
                self.hT = Ring(nc, st, "hT", [128, 8, 128], BF16, 2)
                self.pT = pT
                self.pi = 0

            def next_pT(self):
                it = self.pT[self.pi % len(self.pT)]
                self.pi += 1
                return it

            def norm_T_from_sbuf(self, xt, bx):
                sq, bsq = self.sq.get()
                ss, bss = self.ss.get()
                hn, bhn = self.hn.get()
                hT, bhT = self.hT.get()
                pT, bpT = self.next_pT()
                S.op("act", lambda e: e.activation(out=sq[:], in_=xt[:], func=AF.Square, accum_out=ss[:, 0:1]),
                     reads=[bx], writes=[bsq, bss])
                rsqrt_mean(ss[:, 1:2], ss[:, 0:1], D, [bss], [bss])
                S.op("dve", lambda e: e.tensor_scalar(out=hn[:], in0=xt[:], scalar1=ss[:, 1:2], scalar2=None,
                                                      op0=ALU.mult), reads=[bx, bss], writes=[bhn])
                S.group("pe", [lambda e, k=k: e.transpose(pT[:, k * 128:(k + 1) * 128], hn[:, k * 128:(k + 1) * 128],
                                                          ident[:]) for k in range(8)],
                        reads=[bhn, b_ident], writes=[bpT])
                self.pi2 = getattr(self, "pi2", 0) + 1
                if self.pi2 % 4 != 3:
                    S.op("act", lambda e: e.activation(out=hT[:].rearrange("p a b -> p (a b)"), in_=pT[:], func=AF.Copy),
                         reads=[bpT], writes=[bhT])
                else:
                    S.op("dve", lambda e: e.tensor_copy(hT[:].rearrange("p a b -> p (a b)"), pT[:]), reads=[bpT], writes=[bhT])
                return hT, bhT

            def norm_T(self, src_ap):
                xt, bx = self.xt.get()
                S.dma(xt[:], src_ap, bx, writes=[bx])
                hT, bhT = self.norm_T_from_sbuf(xt, bx)
                return hT, bhT, xt, bx

        def load_w_cols(st, name, src_d, c0, c1, scale_pk, b_scale, stage):
            w = sb(st, name, [128, 8, c1 - c0], BF16)
            bw = Buf(name)
            for k in range(8):
                stg, bstg = stage.get()
                S.dma(stg[:, 0:c1 - c0], src_d[k * 128:(k + 1) * 128, c0:c1], bstg, writes=[bstg])
                eng = "dve" if k % 2 == 0 else "pool"
                S.op(eng, lambda e, k=k, stg=stg: e.tensor_scalar(out=w[:, k, :], in0=stg[:, 0:c1 - c0],
                                                                   scalar1=scale_pk[:, k:k + 1], scalar2=None, op0=ALU.mult),
                     reads=[bstg, b_scale], writes=[bw])
            return w, bw

        def proj(hT, bhT, w, bw, c0, c1, pso, bpso):
            S.group("pe", [lambda e, k=k: e.matmul(pso[:, 0:c1 - c0], lhsT=hT[:, k, :], rhs=w[:, k, c0:c1],
                                                   start=(k == 0), stop=(k == 7)) for k in range(8)],
                    reads=[bhT, bw], writes=[bpso])

        seg_base = []
        main_base = []
        o = 0
        mo_ = 0
        for _, m, h in cfg.segs:
            seg_base.append(o)
            main_base.append(mo_)
            o += m + (2 if h else 0)
            mo_ += m

        def gate_vectors(st_name, G, bG, lf, blf, pcs, bpcs, vec, bvec, flags=None):
            S.op("act", lambda e: e.activation(out=lf[:, 0:16], in_=G[:, 16:32], func=AF.Exp, scale=-1.0), reads=[bG], writes=[blf])
            S.op("act", lambda e: e.activation(out=lf[:, 0:16], in_=lf[:, 0:16], func=AF.Ln, scale=1.0, bias=onet[:]),
                 reads=[blf, b_one], writes=[blf])
            S.op("dve", lambda e: e.tensor_scalar(out=lf[:, 0:16], in0=lf[:, 0:16], scalar1=-1.0, scalar2=None, op0=ALU.mult),
                 reads=[blf], writes=[blf])
            S.group("pe", [lambda e: e.matmul(pcs[:, 0:16], lhsT=trif[:], rhs=lf[:, 0:16], start=True, stop=True),
                           lambda e: e.matmul(pcs[:, 16:32], lhsT=onesf[:], rhs=lf[:, 0:16], start=True, stop=True)],
                    reads=[blf, b_trif, b_onesf], writes=[bpcs])
            S.op("dve", lambda e: e.tensor_copy(lf[:, 16:24], pcs[:, 0:8]), reads=[bpcs, blf], writes=[blf])
            S.op("dve", lambda e: e.tensor_tensor(out=lf[:, 24:32], in0=pcs[:, 24:32], in1=lf[:, 8:16], op=ALU.add),
                 reads=[bpcs, blf], writes=[blf])
            S.op("dve", lambda e: e.tensor_tensor(out=lf[:, 24:32], in0=lf[:, 24:32], in1=pcs[:, 8:16], op=ALU.subtract),
                 reads=[bpcs, blf], writes=[blf])
            S.op("act", lambda e: e.activation(out=vec[:, 0, :], in_=lf[:, 16:32], func=AF.Exp), reads=[blf], writes=[bvec])
            S.op("dve", lambda e: e.tensor_tensor(out=lf[:, 16:32], in0=G[:, 0:16], in1=lf[:, 16:32], op=ALU.subtract),
                 reads=[bG, blf], writes=[blf])
            S.op("act", lambda e: e.activation(out=vec[:, 1, :], in_=lf[:, 16:32], func=AF.Exp), reads=[blf], writes=[bvec])
            if flags is None:
                S.op("act", lambda e: e.activation(out=vec[:, 2, :], in_=pcs[:, 16:32], func=AF.Exp), reads=[bpcs], writes=[bvec])
            else:
                for d_, fl in enumerate(flags):
                    S.op("act", lambda e, d_=d_, fl=fl: e.activation(out=vec[:, 2, d_ * 8:(d_ + 1) * 8], in_=pcs[:, 16 + d_ * 8:24 + d_ * 8],
                                                                     func=AF.Exp, scale=fl), reads=[bpcs, b_flc], writes=[bvec])
            S.op("act", lambda e: e.activation(out=vec[:, 3, :], in_=pcs[:, 16:32], func=AF.Exp), reads=[bpcs], writes=[bvec])
            S.op("dve", lambda e: e.tensor_tensor(out=vec[:, 3, :], in0=vec[:, 3, :], in1=vec[:, 1, :], op=ALU.mult),
                 reads=[bvec], writes=[bvec])
            if flags is not None:
                for d_, fl in enumerate(flags):
                    S.op("dve", lambda e, d_=d_, fl=fl: e.tensor_scalar(out=vec[:, 3, d_ * 8:(d_ + 1) * 8], in0=vec[:, 3, d_ * 8:(d_ + 1) * 8],
                                                                        scalar1=fl, scalar2=None, op0=ALU.mult),
                         reads=[bvec, b_flc], writes=[bvec])

        b_flc = Buf("flc")

        def qk_norm_rope(qk, bqk, nh, wsl, cs, bcs, tmp, btmp, tmp2, btmp2, rs, brs, outb, boutb, eng="dve"):
            v3 = lambda ap: ap.rearrange("p (h d) -> p h d", d=64)
            S.op(eng, lambda e: e.tensor_tensor(out=tmp[:, 0:nh * 64], in0=qk[:, 0:nh * 64], in1=qk[:, 0:nh * 64], op=ALU.mult),
                 reads=[bqk], writes=[btmp])
            S.op("dve", lambda e: e.tensor_reduce(out=rs[:, 0:nh], in_=v3(tmp[:, 0:nh * 64]), axis=AX.X, op=ALU.add),
                 reads=[btmp], writes=[brs])
            rsqrt_mean(rs[:, 0:nh], rs[:, 0:nh], HD, [brs], [brs])
            S.op(eng, lambda e: e.tensor_tensor(out=v3(qk[:, 0:nh * 64]), in0=v3(qk[:, 0:nh * 64]),
                                                in1=rs[:, 0:nh].unsqueeze(2).to_broadcast([128, nh, 64]), op=ALU.mult),
                 reads=[bqk, brs], writes=[bqk])
            S.op(eng, lambda e: e.tensor_tensor(out=qk[:, 0:nh * 64], in0=qk[:, 0:nh * 64], in1=wsl, op=ALU.mult),
                 reads=[bqk, b_wqk], writes=[bqk])
            v5 = lambda ap: ap.rearrange("p (h a t s) -> p h a t s", a=2, t=2, s=16)
            c4 = cs[:, 0:64].rearrange("p (a t s) -> p a t s", a=2, t=2, s=16)
            s4 = cs[:, 64:128].rearrange("p (a t s) -> p a t s", a=2, t=2, s=16)
            for t_ in range(2):
                S.op(eng, lambda e, t_=t_: e.tensor_tensor(
                    out=v5(tmp[:, 0:nh * 64])[:, :, :, t_, :], in0=v5(qk[:, 0:nh * 64])[:, :, :, 1 - t_, :],
                    in1=s4[:, :, t_, :].unsqueeze(1).to_broadcast([128, nh, 2, 16]), op=ALU.mult),
                    reads=[bqk, bcs, btmp], writes=[btmp])
            S.op(eng, lambda e: e.tensor_tensor(out=v3(tmp2[:, 0:nh * 64]), in0=v3(qk[:, 0:nh * 64]),
                                                in1=cs[:, 0:64].unsqueeze(1).to_broadcast([128, nh, 64]), op=ALU.mult),
                 reads=[bqk, bcs], writes=[btmp2])
            S.op(eng, lambda e: e.tensor_tensor(out=outb[:, 0:nh * 64], in0=tmp2[:, 0:nh * 64], in1=tmp[:, 0:nh * 64], op=ALU.add),
                 reads=[btmp, btmp2], writes=[boutb])

        for si, (sname, M, halo) in enumerate(cfg.segs):
            NT = M + (2 if halo else 0)
            base = seg_base[si]
            tok0 = base * 128
            nctx = NCTX if halo else 0
            NK = (nctx + NT)
            with ExitStack() as st:
                kT = sb(st, "kT", [128, NK * 128], BF16); bkT = Buf("kT")
                vA = sb(st, "vA", [128, NK, 2, 65], BF16); bvA = Buf("vA")
                aqT = sb(st, "aqT", [128, 4, 2, NT * 128], BF16); baqT = Buf("aqT")
                S.op("pool", lambda e: e.memset(aqT[:], 0.0), writes=[baqT])
                st_outer = st
                st = ExitStack()
                pT = [(ps(st, f"pT{i}", [128, 1024], BF16), Buf(f"pT{i}", True)) for i in range(2)]
                P_kv = ps(st, "Pkv", [128, 512], F32); bPkv = Buf("Pkv", True)
                P_a = ps(st, "Pa", [128, 512], F32); bPa = Buf("Pa", True)
                P_b = ps(st, "Pb", [128, 512], F32); bPb = Buf("Pb", True)
                P_g = ps(st, "Pg", [128, 512], F32); bPg = Buf("Pg", True)
                P_D = ps(st, "PD", [128, 1024], F32); bPD = [Buf("PDlo", True), Buf("PDhi", True)]
                tc = TileCtx(st, pT)
                wa, bwa = load_w_bf(st, "wa", w_in_s, b_wins, C_AQ, INC)
                qkr = Ring(nc, st, "qk", [128, 640], F32, 3)
                tmpr = Ring(nc, st, "qtmp", [128, 640], F32, 2)
                tmp2r = Ring(nc, st, "qtmp2", [128, 640], F32, 2)
                rsr = Ring(nc, st, "qrs", [128, 16], F32, 2)
                qbr = Ring(nc, st, "qb", [128, 640], BF16, 2)
                csr = Ring(nc, st, "cs", [128, 128], F32, 3)
                if halo and nctx > 0:
                    wm, bwm = load_w_bf(st, "wm", w_in_s, b_wins, C_MK, C_MK + 1024)
                    wg, bwg = load_w_bf(st, "wg", w_in_s, b_wins, C_G, C_G + NG)
                    flc = sb(st, "flc", [128, nctx, 3], F32)
                    nflc = sb(st, "nflc", [128, nctx, 3], F32)
                    S.dma(flc[:].rearrange("p a b -> p (a b)"), flc_d, b_flc, writes=[b_flc])
                    S.op("dve", lambda e: e.tensor_scalar(out=nflc[:], in0=flc[:], scalar1=-1.0, scalar2=None, op0=ALU.mult),
                         reads=[b_flc], writes=[b_flc])
                    Gr = Ring(nc, st, "cG", [128, NG], F32, 3)
                    lfr = Ring(nc, st, "clf", [128, 32], F32, 2)
                    vecr = Ring(nc, st, "cvec", [128, 4, 16], F32, 2)
                    mkr = Ring(nc, st, "cmk", [128, 512], BF16, 3)
                    vhr = Ring(nc, st, "cvh", [128, 8, 65], BF16, 2)
                    vfr = Ring(nc, st, "cvf", [128, 8, 65], F32, 3)
                    for vf_, bvf_ in vfr.items:
                        S.op("pool", lambda e, vf_=vf_: e.memset(vf_[:, :, 64:65], 1.0), writes=[bvf_])
                    Rst = sb(st, "Rst", [128, 2, 8, 65], F32); bR = Buf("Rst")
                    S.op("pool", lambda e: e.memset(Rst[:], 0.0), writes=[bR])

                def att_A(pre, cs_ap, kslot, qslot, valid_ap):
                    hT, bhT, xt, bx = pre
                    cs, bcs = csr.get()
                    S.dma(cs[:], cs_ap, bcs, writes=[bcs])
                    qk, bqk = qkr.get()
                    if qslot is not None:
                        proj(hT, bhT, wa, bwa, 0, 512, P_a[:, 0:512], bPa)
                        S.op("act", lambda e: e.activation(out=qk[:, 0:512], in_=P_a[:, 0:512], func=AF.Copy),
                             reads=[bPa], writes=[bqk])
                    proj(hT, bhT, wa, bwa, 512, 768, P_kv[:, 0:256], bPkv)
                    ko = 512 if qslot is not None else 0
                    S.op("act", lambda e: e.activation(out=qk[:, ko:ko + 128], in_=P_kv[:, 0:128], func=AF.Copy),
                         reads=[bPkv], writes=[bqk])
                    S.op("act", lambda e: e.activation(out=vA[:, kslot, :, 0:64],
                                                       in_=P_kv[:, 128:256].rearrange("p (h d) -> p h d", d=64), func=AF.Copy),
                         reads=[bPkv], writes=[bvA])
                    if valid_ap is None:
                        S.op("pool", lambda e: e.memset(vA[:, kslot, :, 64:65], 1.0), writes=[bvA])
                    else:
                        S.op("pool", lambda e: e.tensor_copy(vA[:, kslot, :, 64:65], valid_ap.unsqueeze(1).to_broadcast([128, 2, 1])),
                             reads=[b_flc, b_flo], writes=[bvA])
                    return (qk, bqk, cs, bcs, kslot, qslot)

                def att_B(hA):
                    qk, bqk, cs, bcs, kslot, qslot = hA
                    tmp, btmp = tmpr.get(); tmp2, btmp2 = tmp2r.get()
                    rs, brs = rsr.get(); qb, bqb = qbr.get()
                    if qslot is not None:
                        qk_norm_rope(qk, bqk, 10, wqk[:, 0:640], cs, bcs, tmp, btmp, tmp2, btmp2, rs, brs, qb, bqb,
                                     eng="dve" if kslot % 2 == 0 else "pool")
                        pT_, bpT_ = tc.next_pT()
                        S.group("pe", [lambda e, j=j: e.transpose(pT_[:, j * 128:(j + 1) * 128], qb[:, j * 128:(j + 1) * 128], ident[:])
                                       for j in range(5)], reads=[bqb, b_ident], writes=[bpT_])
                        S.op("act", lambda e: e.activation(out=aqT[0:64, :, 0, qslot * 128:(qslot + 1) * 128],
                                                           in_=pT_[0:64, 0:512].rearrange("p (j t) -> p j t", t=128), func=AF.Copy),
                             reads=[bpT_], writes=[baqT])
                        S.op("act", lambda e: e.activation(out=aqT[64:128, :, 1, qslot * 128:(qslot + 1) * 128],
                                                           in_=pT_[64:128, 0:512].rearrange("p (j t) -> p j t", t=128), func=AF.Copy),
                             reads=[bpT_], writes=[baqT])
                        S.op("dve", lambda e: e.tensor_copy(kT[:, kslot * 128:(kslot + 1) * 128], pT_[:, 512:640]),
                             reads=[bpT_], writes=[bkT])
                    else:
                        qk_norm_rope(qk, bqk, 2, wqk[:, 512:640], cs, bcs, tmp, btmp, tmp2, btmp2, rs, brs, qb, bqb, eng="pool")
                        pT_, bpT_ = tc.next_pT()
                        S.group("pe", [lambda e: e.transpose(pT_[:, 0:128], qb[:, 0:128], ident[:])], reads=[bqb, b_ident], writes=[bpT_])
                        S.op("act", lambda e: e.activation(out=kT[:, kslot * 128:(kslot + 1) * 128], in_=pT_[:, 0:128], func=AF.Copy),
                             reads=[bpT_], writes=[bkT])

                def ctx_A(ci, pre):
                    hT, bhT, xt, bx = pre
                    G, bG = Gr.get(); mk, bmk = mkr.get(); vf, bvf = vfr.get()
                    proj(hT, bhT, wm, bwm, 0, 512, P_a[:, 0:512], bPa)
                    S.op("act", lambda e: e.activation(out=mk[:], in_=P_a[:, 0:512], func=AF.Copy), reads=[bPa], writes=[bmk])
                    proj(hT, bhT, wm, bwm, 512, 1024, P_b[:, 0:512], bPb)
                    S.op("act", lambda e: e.activation(out=vf[:, :, 0:64], in_=P_b[:, 0:512].rearrange("p (h d) -> p h d", d=64),
                                                       func=AF.Copy), reads=[bPb], writes=[bvf])
                    proj(hT, bhT, wg, bwg, 0, NG, P_g[:, 0:NG], bPg)
                    S.op("dve", lambda e: e.tensor_tensor(out=G[:], in0=P_g[:, 0:NG], in1=bg[:], op=ALU.add),
                         reads=[bPg, b_bg], writes=[bG])
                    return (ci, G, bG, mk, bmk, vf, bvf)

                def ctx_B(hC):
                    ci, G, bG, mk, bmk, vf, bvf = hC
                    lf, blf = lfr.get(); vec, bvec = vecr.get(); vh, bvh = vhr.get()
                    fl = (flc[:, ci, 0:1], flc[:, ci, 1:2])
                    nfl = (nflc[:, ci, 0:1], nflc[:, ci, 1:2])
                    pcs = P_g[:, 64:96]
                    S.op("act", lambda e: e.activation(out=lf[:, 0:16], in_=G[:, 16:32], func=AF.Exp, scale=-1.0), reads=[bG], writes=[blf])
                    S.op("act", lambda e: e.activation(out=lf[:, 0:16], in_=lf[:, 0:16], func=AF.Ln, scale=1.0, bias=onet[:]),
                         reads=[blf, b_one], writes=[blf])
                    S.group("pe", [lambda e: e.matmul(pcs[:, 0:16], lhsT=trif[:], rhs=lf[:, 0:16], start=True, stop=True),
                                   lambda e: e.matmul(pcs[:, 16:32], lhsT=onesf[:], rhs=lf[:, 0:16], start=True, stop=True)],
                            reads=[blf, b_trif, b_onesf], writes=[bPg])
                    S.op("dve", lambda e: e.tensor_tensor(out=vec[:, 0, 0:8], in0=G[:, 0:8], in1=pcs[:, 0:8], op=ALU.add), reads=[bG, bPg], writes=[bvec])
                    S.op("dve", lambda e: e.tensor_tensor(out=vec[:, 0, 0:8], in0=vec[:, 0, 0:8], in1=pcs[:, 16:24], op=ALU.subtract),
                         reads=[bvec, bPg], writes=[bvec])
                    S.op("dve", lambda e: e.tensor_tensor(out=vec[:, 0, 8:16], in0=G[:, 8:16], in1=lf[:, 8:16], op=ALU.add), reads=[bG, blf], writes=[bvec])
                    S.op("dve", lambda e: e.tensor_tensor(out=vec[:, 0, 8:16], in0=vec[:, 0, 8:16], in1=pcs[:, 8:16], op=ALU.subtract),
                         reads=[bvec, bPg], writes=[bvec])
                    S.op("act", lambda e: e.activation(out=vec[:, 1, :], in_=vec[:, 0, :], func=AF.Exp), reads=[bvec], writes=[bvec])
                    for d_ in range(2):
                        S.op("act", lambda e, d_=d_: e.activation(out=vec[:, 2, d_ * 8:(d_ + 1) * 8], in_=pcs[:, 16 + d_ * 8:24 + d_ * 8],
                                                                  func=AF.Exp, scale=nfl[d_]), reads=[bPg, b_flc], writes=[bvec])
                    S.op("dve", lambda e: e.tensor_scalar(out=lf[:, 16:24], in0=vec[:, 1, 0:8], scalar1=fl[0], scalar2=None, op0=ALU.mult),
                         reads=[bvec, b_flc, blf], writes=[blf])
                    S.op("dve", lambda e: e.scalar_tensor_tensor(out=lf[:, 0:8], in0=vec[:, 1, 8:16], scalar=fl[1], in1=lf[:, 16:24],
                                                                 op0=ALU.mult, op1=ALU.add), reads=[bvec, b_flc, blf], writes=[blf])
                    S.op("pool", lambda e: e.tensor_tensor(out=vh[:], in0=vf[:], in1=lf[:, 0:8].unsqueeze(2).to_broadcast([128, 8, 65]),
                                                           op=ALU.mult), reads=[bvf, blf], writes=[bvh])
                    S.group("pe", [lambda e, pr=pr: e.matmul(
                        P_D[:, (pr // 2) * 512 + (pr % 2) * 130:(pr // 2) * 512 + (pr % 2) * 130 + 130],
                        lhsT=mk[:, pr * 128:(pr + 1) * 128],
                        rhs=vh[:, 2 * pr:2 * pr + 2, :].rearrange("p h c -> p (h c)"), start=True, stop=True)
                        for pr in range(4)], reads=[bmk, bvh], writes=bPD)
                    pc4 = P_D[:].rearrange("p (b c) -> p b c", b=2)[:, :, 0:260]
                    for d_ in range(2):
                        Rv = Rst[:, d_].rearrange("p h c -> p (h c)").rearrange("p (b c) -> p b c", b=2)
                        S.op("dve", lambda e, d_=d_: e.tensor_tensor(
                            out=Rst[:, d_], in0=Rst[:, d_], in1=vec[:, 2, d_ * 8:(d_ + 1) * 8].unsqueeze(2).to_broadcast([128, 8, 65]),
                            op=ALU.mult), reads=[bR, bvec], writes=[bR])
                        S.op("dve", lambda e, Rv=Rv, d_=d_: e.scalar_tensor_tensor(out=Rv, in0=pc4, scalar=fl[d_], in1=Rv,
                                                                                   op0=ALU.mult, op1=ALU.add),
                             reads=[bR, b_flc] + bPD, writes=[bR])
                    if ci == nctx - 1:
                        S.op("dve", lambda e: e.tensor_copy(cinit[:], Rst[:]), reads=[bR], writes=[b_cinit])

                srcs = [xc_d[ci] for ci in range(nctx)] + [xs_d[base + t_] for t_ in range(NT)]
                ntile = len(srcs)
                pre, hA, hC = {}, {}, {}
                for i in range(ntile + 2):
                    recs = []
                    if i < ntile:
                        recs.append(S.record(lambda: pre.__setitem__(i, tc.norm_T(srcs[i]))))
                    k_ = i - 1
                    if 0 <= k_ < ntile:
                        p_ = pre.pop(k_)
                        if k_ < nctx:
                            recs.append(S.record(lambda: hA.__setitem__(k_, att_A(p_, csc_d[k_], k_, None, flc[:, k_, 2:3]))))
                            recs.append(S.record(lambda: hC.__setitem__(k_, ctx_A(k_, p_))))
                        else:
                            t_ = k_ - nctx
                            valid = None
                            if halo and t_ >= M:
                                valid = flo[:, (t_ - M):(t_ - M) + 1]
                            recs.append(S.record(lambda: hA.__setitem__(k_, att_A(p_, cso_d[base + t_], nctx + t_, t_, valid))))
                    k2 = i - 2
                    if 0 <= k2 < ntile:
                        recs.append(S.record(lambda: att_B(hA.pop(k2))))
                        if k2 < nctx:
                            recs.append(S.record(lambda: ctx_B(hC.pop(k2))))
                    S.interleave(recs)

                S.flush()
                st.close()
                st = st_outer
                NSB = 3 if NK > 32 else 2
                NOB = 2 if NK > 32 else 4
                Sps = [(ps(st, f"S{i}", [128, 1024], F32), Buf(f"S{i}", True)) for i in range(NSB)]
                Obanks = [(ps(st, f"O{i}", [128, 512], F32), Buf(f"O{i}", True)) for i in range(NOB)]
                PTr = Ring(nc, st, "PT", [128, 2, 512], BF16, 4)
                recr = Ring(nc, st, "rec", [65, 512], F32, 4)
                bcsr = Ring(nc, st, "bcs", [64, 512], F32, 4)
                onr = Ring(nc, st, "on", [64, 512], BF16, 4)
                groups = []
                q0 = 0
                while q0 < M * 128:
                    gq = min(cfg.GT * 128, M * 128 - q0)
                    groups.append((q0, gq))
                    q0 += gq
                if halo:
                    groups.append((M * 128 + 127, 2))
                    zt_ = sb(st, "zhalo", [128, 256], BF16); bzt_ = Buf("zhalo")
                    S.op("pool", lambda e: e.memset(zt_[:], 0.0), writes=[bzt_])
                    for r_ in range(4):
                        S.dma(mixT_d[512 + r_ * 128:512 + (r_ + 1) * 128, tok0 + M * 128:tok0 + M * 128 + 256], zt_[:], bzt_,
                              reads=[bzt_], writes=[b_mixT])
                units = [(q0, gq, j, half) for (q0, gq) in groups for j in range(4) for half in range(2)]
                work = [(ui, kt, min(2, NK - kt)) for ui in range(len(units)) for kt in range(0, NK, 2)]
                pts = {}

                def emit_scores(wi):
                    ui, kt0, nk = work[wi]
                    q0, gq, j, half = units[ui]
                    hs = slice(64 * half, 64 * half + 64)
                    pS, bpS = Sps[wi % NSB]
                    PT, bPT = PTr.get()
                    pts[wi] = (PT, bPT)
                    S.group("pe", [lambda e, i=i: e.matmul(pS[:, i * 512:i * 512 + gq], lhsT=kT[:, (kt0 + i) * 128:(kt0 + i + 1) * 128],
                                                           rhs=aqT[:, j, half, q0:q0 + gq], start=True, stop=True) for i in range(nk)],
                            reads=[bkT, baqT], writes=[bpS])
                    S.op("act", lambda e: e.activation(out=PT[:, 0:nk, 0:gq], in_=pS[:].rearrange("p (a b) -> p a b", a=2)[:, 0:nk, 0:gq],
                                                       func=AF.Exp, bias=negc[:], scale=1.0),
                         reads=[bpS, b_negc], writes=[bPT])

                def emit_pv(wi):
                    ui, kt0, nk = work[wi]
                    q0, gq, j, half = units[ui]
                    po, bpo = Obanks[ui % NOB]
                    PT, bPT = pts.pop(wi)
                    S.group("pe", [lambda e, i=i: e.matmul(po[0:65, 0:gq], lhsT=vA[:, kt0 + i, half, :], rhs=PT[:, i, 0:gq],
                                                           start=(kt0 + i == 0), stop=(kt0 + i == NK - 1)) for i in range(nk)],
                            reads=[bvA, bPT], writes=[bpo])
                    if kt0 + nk == NK:
                        rec, brec = recr.get(); bcs_, bbcs = bcsr.get(); on, bon = onr.get()
                        S.op("dve", lambda e: e.reciprocal(rec[64:65, 0:gq], po[64:65, 0:gq]), reads=[bpo], writes=[brec])
                        slot = ui % 4
                        S.dma(recd_d[slot:slot + 1, 0:gq], rec[64:65, 0:gq], b_recd[slot], reads=[brec], writes=[b_recd[slot]])
                        S.dma(bcs_[:, 0:gq], recd_d[slot:slot + 1, 0:gq].to_broadcast([64, gq]), bbcs,
                              reads=[b_recd[slot]], writes=[bbcs])
                        S.op("dve", lambda e: e.tensor_tensor(out=on[:, 0:gq], in0=po[0:64, 0:gq], in1=bcs_[:, 0:gq],
                                                              op=ALU.mult), reads=[bpo, bbcs], writes=[bon])
                        r0 = 512 + (2 * j + half) * 64
                        S.dma(mixT_d[r0:r0 + 64, tok0 + q0:tok0 + q0 + gq], on[:, 0:gq], bon, reads=[bon], writes=[b_mixT])

                LOOK = NSB - 1

                def att_stream():
                    for step in range(len(work) + LOOK):
                        if step < len(work):
                            emit_scores(step)
                        if step - LOOK >= 0:
                            emit_pv(step - LOOK)
                if si == 0:
                    conv_bg = make_conv(st, "pool")
                    S.interleave_spread(S.record(att_stream), S.record(lambda: conv_ffn_weights(conv_bg)))
                else:
                    att_stream()
                S.flush()

            with ExitStack() as st:
                pT = [(ps(st, f"pT{i}", [128, 1024], BF16), Buf(f"pT{i}", True)) for i in range(2)]
                Dp = [ps(st, f"D{i}", [128, 1024], F32) for i in range(3)]
                bD = [[Buf(f"D{i}lo", True), Buf(f"D{i}hi", True)] for i in range(3)]
                qT = sb(st, "mqT", [128, 4, NT * 128], BF16); bqT = Buf("mqT")
                kTm = sb(st, "mkT", [128, 4, NT * 128], BF16); bkTm = Buf("mkT")
                mkA = sb(st, "mkA", [128, NT, 512], BF16); bmkA = Buf("mkA")
                mvA = sb(st, "mvA", [128, NT, 8, 65], BF16); bmvA = Buf("mvA")
                sgA = sb(st, "sgA", [128, NT, 512], BF16); bsgA = Buf("sgA")
                vecA = sb(st, "vecA", [128, NT, 4, 16], F32)
                bvecA = [Buf(f"vecA{t_}") for t_ in range(NT)]
                hfA = sb(st, "hfA", [128, NT, 512], BF16); bhfA = [Buf(f"hfA{t_}") for t_ in range(NT)]
                maskfb = sb(st, "maskfb", [128, 128], BF16); maskbb = sb(st, "maskbb", [128, 128], BF16)
                st_outer = st
                st = ExitStack()
                tc = TileCtx(st, pT, nb=1)
                wmm, bwmm = load_w_bf(st, "wmm", w_in_s, b_wins, 0, 2080)
                mqb_r = Ring(nc, st, "mqb", [128, 1024], BF16, 1)
                G_all = sb(st, "mGall", [128, NT, NG], F32); bGall = Buf("mGall")
                lf_all = sb(st, "mlfall", [128, NT, 32], F32); blfall = Buf("mlfall")
                sg_r = Ring(nc, st, "sgt", [128, 512], F32, 1)
                S.op("pool", lambda e: e.memset(mvA[:, :, :, 64:65], 1.0), writes=[bmvA])
                def mproj_tile(t_, pre):
                    hT, bhT, xt, bx = pre
                    mqb, bmqb = mqb_r.get()
                    proj(hT, bhT, wmm, bwmm, 0, 512, Dp[0][:, 0:512], bD[0][0])
                    S.op("act", lambda e, mqb=mqb: e.activation(out=mqb[:, 0:512], in_=Dp[0][:, 0:512], func=AF.Copy), reads=[bD[0][0]], writes=[bmqb])
                    proj(hT, bhT, wmm, bwmm, 512, 1024, Dp[0][:, 512:1024], bD[0][1])
                    S.op("act", lambda e, mqb=mqb: e.activation(out=mqb[:, 512:1024], in_=Dp[0][:, 512:1024], func=AF.Copy), reads=[bD[0][1]], writes=[bmqb])
                    S.op("pool", lambda e, mqb=mqb, t_=t_: e.tensor_copy(mkA[:, t_, :], mqb[:, 512:1024]), reads=[bmqb], writes=[bmkA])
                    pT_, bpT_ = tc.next_pT()
                    S.group("pe", [lambda e, j=j, mqb=mqb, pT_=pT_: e.transpose(pT_[:, j * 128:(j + 1) * 128], mqb[:, j * 128:(j + 1) * 128], ident[:])
                                   for j in range(8)], reads=[bmqb, b_ident], writes=[bpT_])
                    S.op("act", lambda e, pT_=pT_, t_=t_: e.activation(out=qT[:, :, t_ * 128:(t_ + 1) * 128],
                                                                      in_=pT_[:, 0:512].rearrange("p (j t) -> p j t", t=128), func=AF.Copy),
                         reads=[bpT_], writes=[bqT])
                    S.op("dve", lambda e, pT_=pT_, t_=t_: e.tensor_copy(kTm[:, :, t_ * 128:(t_ + 1) * 128],
                                                                       pT_[:, 512:1024].rearrange("p (j t) -> p j t", t=128)),
                         reads=[bpT_], writes=[bkTm])

                def mproj_tile2(t_, pre):
                    hT, bhT, xt, bx = pre
                    sgt, bsgt = sg_r.get()
                    proj(hT, bhT, wmm, bwmm, 1024, 1536, Dp[1][:, 0:512], bD[1][0])
                    S.op("act", lambda e, t_=t_: e.activation(out=mvA[:, t_, :, 0:64], in_=Dp[1][:, 0:512].rearrange("p (h d) -> p h d", d=64),
                                                              func=AF.Copy), reads=[bD[1][0]], writes=[bmvA])
                    proj(hT, bhT, wmm, bwmm, 1536, 2048, Dp[1][:, 512:1024], bD[1][1])
                    S.op("act", lambda e, sgt=sgt: e.activation(out=sgt[:], in_=Dp[1][:, 512:1024], func=AF.Exp, scale=-1.0),
                         reads=[bD[1][1]], writes=[bsgt])
                    S.op("pool", lambda e, sgt=sgt: e.tensor_scalar(out=sgt[:], in0=sgt[:], scalar1=1.0, scalar2=None, op0=ALU.add),
                         reads=[bsgt], writes=[bsgt])
                    S.op("dve", lambda e, sgt=sgt, t_=t_: e.reciprocal(sgA[:, t_, :], sgt[:]), reads=[bsgt], writes=[bsgA])
                    proj(hT, bhT, wmm, bwmm, 2048, 2080, Dp[2][:, 0:NG], bD[2][0])
                    S.op("dve", lambda e: e.tensor_tensor(out=G_all[:, t_, :], in0=Dp[2][:, 0:NG], in1=bg[:], op=ALU.add),
                         reads=[bD[2][0], b_bg], writes=[bGall])
                pre = {}
                for i in range(NT + 1):
                    recs = []
                    if i < NT:
                        recs.append(S.record(lambda: pre.__setitem__(i, tc.norm_T(xs_d[base + i]))))
                    if i >= 1:
                        p_ = pre.pop(i - 1)
                        recs.append(S.record(lambda: mproj_tile(i - 1, p_)))
                        recs.append(S.record(lambda: mproj_tile2(i - 1, p_)))
                    S.interleave(recs)
                G3, lf3 = G_all, lf_all
                S.op("act", lambda e: e.activation(out=lf3[:, :, 0:16], in_=G3[:, :, 16:32], func=AF.Exp, scale=-1.0), reads=[bGall], writes=[blfall])
                S.op("act", lambda e: e.activation(out=lf3[:, :, 0:16], in_=lf3[:, :, 0:16], func=AF.Ln, scale=1.0, bias=onet[:]),
                     reads=[blfall, b_one], writes=[blfall])
                S.op("dve", lambda e: e.tensor_scalar(out=lf3[:, :, 0:16], in0=lf3[:, :, 0:16], scalar1=-1.0, scalar2=None, op0=ALU.mult),
                     reads=[blfall], writes=[blfall])
                fns = []
                for t_ in range(NT):
                    fns.append(lambda e, t_=t_: e.matmul(Dp[2][:, t_ * 32:t_ * 32 + 16], lhsT=trif[:], rhs=lf3[:, t_, 0:16], start=True, stop=True))
                    fns.append(lambda e, t_=t_: e.matmul(Dp[2][:, t_ * 32 + 16:t_ * 32 + 32], lhsT=onesf[:], rhs=lf3[:, t_, 0:16], start=True, stop=True))
                S.group("pe", fns, reads=[blfall, b_trif, b_onesf], writes=bD[2])
                pcs3 = Dp[2][:, 0:NT * 32].rearrange("p (t c) -> p t c", c=32)
                S.op("dve", lambda e: e.tensor_copy(lf3[:, :, 16:24], pcs3[:, :, 0:8]), reads=bD[2] + [blfall], writes=[blfall])
                S.op("dve", lambda e: e.tensor_tensor(out=lf3[:, :, 24:32], in0=pcs3[:, :, 24:32], in1=lf3[:, :, 8:16], op=ALU.add),
                     reads=bD[2] + [blfall], writes=[blfall])
                S.op("dve", lambda e: e.tensor_tensor(out=lf3[:, :, 24:32], in0=lf3[:, :, 24:32], in1=pcs3[:, :, 8:16], op=ALU.subtract),
                     reads=bD[2] + [blfall], writes=[blfall])
                S.op("act", lambda e: e.activation(out=vecA[:, :, 0, :], in_=lf3[:, :, 16:32], func=AF.Exp), reads=[blfall], writes=bvecA)
                S.op("dve", lambda e: e.tensor_tensor(out=lf3[:, :, 16:32], in0=G3[:, :, 0:16], in1=lf3[:, :, 16:32], op=ALU.subtract),
                     reads=[bGall, blfall], writes=[blfall])
                S.op("act", lambda e: e.activation(out=vecA[:, :, 1, :], in_=lf3[:, :, 16:32], func=AF.Exp), reads=[blfall], writes=bvecA)
                S.op("act", lambda e: e.activation(out=vecA[:, :, 2, :], in_=pcs3[:, :, 16:32], func=AF.Exp), reads=bD[2], writes=bvecA)
                S.op("dve", lambda e: e.tensor_tensor(out=vecA[:, :, 3, :], in0=vecA[:, :, 2, :], in1=vecA[:, :, 1, :], op=ALU.mult),
                     reads=bvecA, writes=bvecA)
                S.op("dve", lambda e: e.tensor_copy(maskfb[:], maskf[:]), reads=[b_maskf], writes=[b_maskf])
                S.op("dve", lambda e: e.tensor_copy(maskbb[:], maskb[:]), reads=[b_maskb], writes=[b_maskb])

                S.flush()
                st.close()
                st = st_outer
                pTi = [0]

                def next_pT():
                    it = pT[pTi[0] % 2]
                    pTi[0] += 1
                    return it
                Cst = sb(st, "Cst", [128, 8, 65], F32); bC = Buf("Cst")
                Cbf_r = Ring(nc, st, "Cbf", [128, 8, 65], BF16, 3)
                cbh = {}
                PTm_r = Ring(nc, st, "PTm", [128, 8, 128], BF16, 2)
                vt_r = Ring(nc, st, "vt", [128, 8, 65], BF16, 2)
                vh_r = Ring(nc, st, "vhm", [128, 8, 65], BF16, 2)
                r_r = Ring(nc, st, "rr", [128, 24], F32, 2)
                hs_r = Ring(nc, st, "hs", [128, 512], F32, 2)
                h2_r = Ring(nc, st, "hsq", [128, 512], F32, 2)
                rs_r = Ring(nc, st, "mrs", [128, 8], F32, 2)
                mo_r = Ring(nc, st, "mout", [128, 512], BF16, 2)
                mT_r = Ring(nc, st, "mTt", [128, 4, 128], BF16, 2)
                seq = ([M] if halo else []) + list(range(M)) + ([M + 1] if halo else [])
                for d_ in (1, 0):
                    order = seq if d_ == 0 else seq[::-1]
                    mask = maskfb if d_ == 0 else maskbb
                    if halo:
                        S.op("dve", lambda e, d_=d_: e.tensor_copy(Cst[:], cinit[:, d_]), reads=[b_cinit], writes=[bC])
                    else:
                        S.op("pool", lambda e: e.memset(Cst[:], 0.0), writes=[bC])
                    Cbf0, bCbf0 = Cbf_r.get()
                    S.op("act", lambda e, Cbf0=Cbf0: e.activation(out=Cbf0[:], in_=Cst[:], func=AF.Copy), reads=[bC], writes=[bCbf0])
                    cbh["cur"] = (Cbf0, bCbf0)
                    def scan_S(t_, d_=d_):
                        vsl = lambda k_: vecA[:, t_, k_, d_ * 8:(d_ + 1) * 8]
                        vh, bvh = vh_r.get()
                        S.op("pool", lambda e, vh=vh, t_=t_, vsl=vsl: e.tensor_tensor(
                            out=vh[:], in0=mvA[:, t_], in1=vsl(3).unsqueeze(2).to_broadcast([128, 8, 65]), op=ALU.mult),
                            reads=[bmvA, bvecA[t_]], writes=[bvh])
                        S.group("pe", [lambda e, pr=pr, vh=vh, t_=t_: e.matmul(
                            Dp[2][:, (pr // 2) * 512 + (pr % 2) * 130:(pr // 2) * 512 + (pr % 2) * 130 + 130],
                            lhsT=mkA[:, t_, pr * 128:(pr + 1) * 128],
                            rhs=vh[:, 2 * pr:2 * pr + 2, :].rearrange("p h c -> p (h c)"), start=True, stop=True)
                            for pr in range(4)], reads=[bmkA, bvh], writes=bD[2])
                        S.op("dve", lambda e, vsl=vsl: e.tensor_tensor(out=Cst[:], in0=Cst[:], in1=vsl(2).unsqueeze(2).to_broadcast([128, 8, 65]),
                                                                       op=ALU.mult), reads=[bC, bvecA[t_]], writes=[bC])
                        Cv = Cst[:].rearrange("p h c -> p (h c)").rearrange("p (b c) -> p b c", b=2)
                        pc4 = Dp[2][:].rearrange("p (b c) -> p b c", b=2)[:, :, 0:260]
                        S.op("dve", lambda e, Cv=Cv, pc4=pc4: e.tensor_tensor(out=Cv, in0=Cv, in1=pc4, op=ALU.add), reads=[bC] + bD[2], writes=[bC])
                        Cbn, bCbn = Cbf_r.get()
                        S.op("act", lambda e: e.activation(out=Cbn[:], in_=Cst[:], func=AF.Copy), reads=[bC], writes=[bCbn])
                        return (Cbn, bCbn)

                    def scan_H(t_, cprev, d_=d_, mask=mask):
                        Cbf, bCbf = cprev
                        ts = slice(t_ * 128, (t_ + 1) * 128)
                        vsl = lambda k_: vecA[:, t_, k_, d_ * 8:(d_ + 1) * 8]
                        PTm, bPTm = PTm_r.get(); vt, bvt = vt_r.get(); rr, brr = r_r.get()
                        S.op("pool", lambda e, vt=vt, t_=t_, vsl=vsl: e.tensor_tensor(
                            out=vt[:], in0=mvA[:, t_], in1=vsl(1).unsqueeze(2).to_broadcast([128, 8, 65]), op=ALU.mult),
                            reads=[bmvA, bvecA[t_]], writes=[bvt])
                        S.group("pe", [lambda e, h=h, ts=ts: e.matmul(Dp[0][:, ((h % 2) * 4 + h // 2) * 128:((h % 2) * 4 + h // 2 + 1) * 128],
                                                                     lhsT=kTm[64 * (h % 2):64 * (h % 2) + 64, h // 2, ts],
                                                                     rhs=qT[64 * (h % 2):64 * (h % 2) + 64, h // 2, ts], start=True, stop=True)
                                       for h in range(8)], reads=[bkTm, bqT], writes=bD[0])
                        S.op("dve", lambda e, PTm=PTm, mask=mask: e.tensor_tensor(
                            out=PTm[:], in0=Dp[0][:].rearrange("p (h l) -> p h l", h=8),
                            in1=mask[:].unsqueeze(1).to_broadcast([128, 8, 128]), op=ALU.mult),
                            reads=bD[0] + [b_maskf, b_maskb], writes=[bPTm])
                        fns = []
                        for h in range(8):
                            c0 = (h // 4) * 512 + (h % 4) * 65
                            hp = slice(64 * (h % 2), 64 * (h % 2) + 64)
                            fns.append(lambda e, h=h, c0=c0, PTm=PTm, vt=vt: e.matmul(Dp[1][:, c0:c0 + 65], lhsT=PTm[:, (h % 2) * 4 + h // 2, :], rhs=vt[:, h, :],
                                                                                     start=True, stop=False))
                            fns.append(lambda e, h=h, c0=c0, hp=hp, ts=ts: e.matmul(Dp[1][:, c0:c0 + 65], lhsT=qT[hp, h // 2, ts],
                                                                                   rhs=Cbf[hp, h, :], start=False, stop=True))
                        S.group("pe", fns, reads=[bPTm, bvt, bqT, bCbf], writes=bD[1])
                        nd = Dp[1][:].rearrange("p (b c) -> p b c", b=2)[:, :, 0:260].rearrange("p b (g c) -> p b g c", c=65)
                        a8 = vsl(0).rearrange("p (b g) -> p b g", b=2)
                        r8 = rr[:, 0:8].rearrange("p (b g) -> p b g", b=2)
                        t8 = rr[:, 8:16].rearrange("p (b g) -> p b g", b=2)
                        S.op("dve", lambda e, nd=nd, a8=a8, t8=t8: e.tensor_tensor(out=t8, in0=nd[:, :, :, 64], in1=a8, op=ALU.mult),
                             reads=bD[1] + [bvecA[t_]], writes=[brr])
                        S.op("dve", lambda e, rr=rr: e.tensor_scalar(out=rr[:, 16:24], in0=rr[:, 8:16], scalar1=-1.0, scalar2=None,
                                                                     op0=ALU.mult), reads=[brr], writes=[brr])
                        S.op("dve", lambda e, rr=rr: e.tensor_tensor(out=rr[:, 8:16], in0=rr[:, 8:16], in1=rr[:, 16:24], op=ALU.max),
                             reads=[brr], writes=[brr])
                        S.op("dve", lambda e, rr=rr: e.tensor_scalar(out=rr[:, 8:16], in0=rr[:, 8:16], scalar1=1.0, scalar2=None,
                                                                     op0=ALU.max), reads=[brr], writes=[brr])
                        S.op("dve", lambda e, rr=rr: e.reciprocal(rr[:, 8:16], rr[:, 8:16]), reads=[brr], writes=[brr])
                        S.op("dve", lambda e, a8=a8, t8=t8, r8=r8: e.tensor_tensor(out=r8, in0=a8, in1=t8, op=ALU.mult),
                             reads=[brr, bvecA[t_]], writes=[brr])
                        if d_ == 1:
                            for b_ in range(2):
                                S.op("dve", lambda e, b_=b_, nd=nd, rr=rr, t_=t_: e.tensor_tensor(
                                    out=hfA[:, t_, b_ * 256:(b_ + 1) * 256].rearrange("p (g d) -> p g d", d=64),
                                    in0=nd[:, b_, :, 0:64], in1=rr[:, b_ * 4:(b_ + 1) * 4].unsqueeze(2).to_broadcast([128, 4, 64]), op=ALU.mult),
                                    reads=bD[1] + [brr], writes=[bhfA[t_]])
                        else:
                            hs, bhs = hs_r.get(); hq, bhq = h2_r.get(); rs, brs = rs_r.get(); mo, bmo = mo_r.get(); mTt, bmTt = mT_r.get()
                            for b_ in range(2):
                                S.op("dve", lambda e, b_=b_, nd=nd, rr=rr, hs=hs: e.tensor_tensor(
                                    out=hs[:, b_ * 256:(b_ + 1) * 256].rearrange("p (g d) -> p g d", d=64),
                                    in0=nd[:, b_, :, 0:64], in1=rr[:, b_ * 4:(b_ + 1) * 4].unsqueeze(2).to_broadcast([128, 4, 64]), op=ALU.mult),
                                    reads=bD[1] + [brr], writes=[bhs])
                            S.op("pool", lambda e, hs=hs, t_=t_: e.tensor_tensor(out=hs[:], in0=hs[:], in1=hfA[:, t_, :], op=ALU.add),
                                 reads=[bhs, bhfA[t_]], writes=[bhs])
                            S.op("pool", lambda e, hs=hs, hq=hq: e.tensor_tensor(out=hq[:], in0=hs[:], in1=hs[:], op=ALU.mult),
                                 reads=[bhs], writes=[bhq])
                            S.op("dve", lambda e, hq=hq, rs=rs: e.tensor_reduce(out=rs[:], in_=hq[:].rearrange("p (h d) -> p h d", d=64),
                                                                                 axis=AX.X, op=ALU.add), reads=[bhq], writes=[brs])
                            rsqrt_mean(rs[:], rs[:], HD, [brs], [brs])
                            S.op("pool", lambda e, hs=hs, rs=rs: e.tensor_tensor(
                                out=hs[:].rearrange("p (h d) -> p h d", d=64), in0=hs[:].rearrange("p (h d) -> p h d", d=64),
                                in1=rs[:].unsqueeze(2).to_broadcast([128, 8, 64]), op=ALU.mult), reads=[bhs, brs], writes=[bhs])
                            S.op("pool", lambda e, hs=hs: e.tensor_tensor(out=hs[:], in0=hs[:], in1=mhw[:], op=ALU.mult),
                                 reads=[bhs, b_mhw], writes=[bhs])
                            S.op("dve", lambda e, hs=hs, mo=mo, t_=t_: e.tensor_tensor(out=mo[:], in0=hs[:], in1=sgA[:, t_, :], op=ALU.mult),
                                 reads=[bhs, bsgA], writes=[bmo])
                            pT_, bpT_ = next_pT()
                            S.group("pe", [lambda e, j=j, mo=mo, pT_=pT_: e.transpose(pT_[:, j * 128:(j + 1) * 128], mo[:, j * 128:(j + 1) * 128], ident[:])
                                           for j in range(4)], reads=[bmo, b_ident], writes=[bpT_])
                            S.op("act", lambda e, pT_=pT_, mTt=mTt: e.activation(out=mTt[:].rearrange("p a b -> p (a b)"), in_=pT_[:, 0:512], func=AF.Copy),
                                 reads=[bpT_], writes=[bmTt])
                            S.dma(mixT_d[0:512, tok0 + t_ * 128:tok0 + (t_ + 1) * 128].rearrange("(j p) t -> p j t", p=128), mTt[:], bmTt,
                                  reads=[bmTt], writes=[b_mixT])
                    cbs = {-1: cbh["cur"]}
                    no = len(order)
                    for i in range(no + 1):
                        recs = []
                        if i < no:
                            recs.append(S.record(lambda: cbs.__setitem__(i, scan_S(order[i]))))
                        if i >= 1:
                            c_ = cbs.pop(i - 2)
                            recs.append(S.record(lambda: scan_H(order[i - 1], c_)))
                        S.interleave(recs)
                S.flush()

            with ExitStack() as st:
                pT = [(ps(st, f"pT{i}", [128, 1024], BF16), Buf(f"pT{i}", True)) for i in range(2)]
                Dp = [ps(st, f"D{i}", [128, 1024], F32) for i in range(3)]
                bD = [[Buf(f"D{i}lo", True), Buf(f"D{i}hi", True)] for i in range(3)]
                tc = TileCtx(st, pT)
                wom = sb(st, "wom", [128, 4, D], BF16); bwom = Buf("wom")
                woa = sb(st, "woa", [64, 8, D], BF16); bwoa = Buf("woa")
                S.dma(wom[:], w_out_s[0:512, :].rearrange("(k p) c -> p k c", p=128), bwom, reads=[b_wouts], writes=[bwom])
                S.dma(woa[:], w_out_s[512:1024, :].rearrange("(k p) c -> p k c", p=64), bwoa, reads=[b_wouts], writes=[bwoa])
                mm_r = Ring(nc, st, "mm", [128, 4, 512], BF16, 2)
                ma_r = Ring(nc, st, "ma", [64, 8, 512], BF16, 2)
                x1_r = Ring(nc, st, "x1", [128, D], F32, 2)
                mixg = {}

                def out_tile(t_):
                    g4 = t_ // 4
                    if g4 not in mixg:
                        n4 = min(4, NT - g4 * 4)
                        mmg, bmmg = mm_r.get(); mag, bmag = ma_r.get()
                        cs4 = slice(tok0 + g4 * 512, tok0 + g4 * 512 + n4 * 128)
                        S.dma(mmg[:, :, 0:n4 * 128], mixT_d[0:512, cs4].rearrange("(j p) t -> p j t", p=128), bmmg, reads=[b_mixT], writes=[bmmg])
                        S.dma(mag[:, :, 0:n4 * 128], mixT_d[512:1024, cs4].rearrange("(j p) t -> p j t", p=64), bmag, reads=[b_mixT], writes=[bmag])
                        mixg.clear()
                        mixg[g4] = (mmg, bmmg, mag, bmag)
                    mmg, bmm, mag, bma = mixg[g4]
                    o4 = (t_ % 4) * 128
                    mm = mmg[:, :, o4:o4 + 128]
                    ma = mag[:, :, o4:o4 + 128]
                    x1, bx1 = x1_r.get()
                    xt, bx = tc.xt.get()
                    S.dma(xt[:], xs_d[base + t_], bx, writes=[bx])
                    dq = t_ % 2
                    for cb in range(2):
                        fns = [lambda e, k=k, cb=cb, mm=mm: e.matmul(Dp[dq][:, cb * 512:(cb + 1) * 512], lhsT=mm[:, k, :],
                                                                     rhs=wom[:, k, cb * 512:(cb + 1) * 512], start=(k == 0), stop=False) for k in range(4)]
                        fns += [lambda e, k=k, cb=cb, ma=ma: e.matmul(Dp[dq][:, cb * 512:(cb + 1) * 512], lhsT=ma[:, k, :],
                                                                      rhs=woa[:, k, cb * 512:(cb + 1) * 512], start=False, stop=(k == 7)) for k in range(8)]
                        S.group("pe", fns, reads=[bmm, bma, bwom, bwoa], writes=[bD[dq][cb]])
                    S.op("dve", lambda e, x1=x1, xt=xt, dq=dq: e.tensor_tensor(out=x1[:], in0=Dp[dq][:], in1=xt[:], op=ALU.add),
                         reads=bD[dq] + [bx], writes=[bx1])
                    if t_ < M:
                        gi = main_base[si] + t_
                        S.dma(x1_d[gi], x1[:], bx1, reads=[bx1], writes=[b_x1d[gi]])
                    return (x1, bx1)

                def out_tile_B(t_, hx):
                    x1, bx1 = hx
                    h2T, bh2T = tc.norm_T_from_sbuf(x1, bx1)
                    if t_ < M:
                        S.dma(h2T_d[si][:, 1 + t_ * 128:1 + (t_ + 1) * 128].rearrange("(k p) t -> p k t", p=128), h2T[:], bh2T,
                              reads=[bh2T], writes=[b_h2d[si]])
                        if not halo and t_ == 0:
                            S.dma(h2T_d[si][:, 0:1].rearrange("(k p) t -> p k t", p=128), h2T[:, :, 0:1], bh2T, reads=[bh2T], writes=[b_h2d[si]])
                        if not halo and t_ == M - 1:
                            S.dma(h2T_d[si][:, M * 128 + 1:M * 128 + 2].rearrange("(k p) t -> p k t", p=128), h2T[:, :, 127:128], bh2T,
                                  reads=[bh2T], writes=[b_h2d[si]])
                    elif t_ == M:
                        S.dma(h2T_d[si][:, 0:1].rearrange("(k p) t -> p k t", p=128), h2T[:, :, 127:128], bh2T, reads=[bh2T], writes=[b_h2d[si]])
                    else:
                        S.dma(h2T_d[si][:, M * 128 + 1:M * 128 + 2].rearrange("(k p) t -> p k t", p=128), h2T[:, :, 0:1], bh2T,
                              reads=[bh2T], writes=[b_h2d[si]])
                hx = {}
                for i in range(NT + 1):
                    recs = []
                    if i < NT:
                        recs.append(S.record(lambda: hx.__setitem__(i, out_tile(i))))
                    if i >= 1:
                        h_ = hx.pop(i - 1)
                        recs.append(S.record(lambda: out_tile_B(i - 1, h_)))
                    S.interleave(recs)
                S.flush()

        with ExitStack() as st:
            Dp = [ps(st, f"D{i}", [128, 1024], F32) for i in range(4)]
            bD = [[Buf(f"D{i}lo", True), Buf(f"D{i}hi", True)] for i in range(4)]
            wup = sb(st, "wup", [128, 8, 2 * DFF], BF16); bwup = Buf("wup")
            wdn = sb(st, "wdn", [128, NPAIR, D], BF16); bwdn = Buf("wdn")
            for k in range(8):
                S.dma(wup[:, k, :], w_up_s[k * 128:(k + 1) * 128, :], bwup, reads=[b_wups], writes=[bwup])
            S.dma(wdn[:], w_dn_s.rearrange("(j p) c -> p j c", p=128), bwdn, reads=[b_wdns], writes=[bwdn])
            cw = sb(st, "cw", [128, NFC, 4], F32); bcw = Buf("cw")
            S.dma(cw[:].rearrange("p a b -> p (a b)"), cw_d, bcw, writes=[bcw])
            GW = cfg.GT * 128
            h2w_r = Ring(nc, st, "h2win", [128, 8, GW + 2], BF16, 1)
            uh_r = Ring(nc, st, "uh", [128, NFC, 2], F32, 1)
            fg_r = Ring(nc, st, "fg", [128, 2], F32, 2)
            U_r = Ring(nc, st, "Ub", [128, GW + 2], F32, 4)
            acc_r = Ring(nc, st, "acc", [128, GW], F32, 5)
            actT = sb(st, "actT", [128, NPAIR, GW], BF16); bact = Buf("actT")
            x1_r = Ring(nc, st, "fx1", [128, D], F32, 2)
            sq_r = Ring(nc, st, "fsq", [128, D], BF16, 1)
            ss_r = Ring(nc, st, "fss", [128, 2], F32, 2)
            pic = [0]
            for si, (sname, M, halo) in enumerate(cfg.segs):
                ngr = (M + cfg.GT - 1) // cfg.GT
                def ffn_group(g, si=si, M=M, halo=halo, ngr=ngr):
                    t0 = g * cfg.GT
                    ntl = min(cfg.GT, M - t0)
                    gw = ntl * 128
                    h2win, bh2w = h2w_r.get(); uh, buh = uh_r.get(); fg, bfg = fg_r.get()
                    S.dma(h2win[:, :, 0:gw + 2], h2T_d[si][:, t0 * 128:t0 * 128 + gw + 2].rearrange("(k p) t -> p k t", p=128), bh2w,
                          reads=[b_h2d[si]], writes=[bh2w])
                    S.op("pool", lambda e, fg=fg: e.memset(fg[:], 1.0 if halo else 0.0), writes=[bfg])
                    if halo and g == 0:
                        S.op("pool", lambda e, fg=fg: e.tensor_copy(fg[:, 0:1], flo[:, 0:1]), reads=[b_flo, bfg], writes=[bfg])
                    if halo and g == ngr - 1:
                        S.op("pool", lambda e, fg=fg: e.tensor_copy(fg[:, 1:2], flo[:, 1:2]), reads=[b_flo, bfg], writes=[bfg])
                    if not halo:
                        if g > 0:
                            S.op("pool", lambda e, fg=fg: e.memset(fg[:, 0:1], 1.0), reads=[bfg], writes=[bfg])
                        if g < ngr - 1:
                            S.op("pool", lambda e, fg=fg: e.memset(fg[:, 1:2], 1.0), reads=[bfg], writes=[bfg])
                    fns = []
                    for c in range(NFC):
                        for k in range(8):
                            fns.append(lambda e, c=c, k=k, h2win=h2win, gw=gw: e.matmul(Dp[3][:, 2 * c:2 * c + 2], lhsT=wup[:, k, c * 128:(c + 1) * 128],
                                                                                       rhs=h2win[:, k, 0:gw + 2:gw + 1], start=(k == 0), stop=(k == 7)))
                    S.group("pe", fns, reads=[bh2w, bwup], writes=[bD[3][0]])
                    S.op("dve", lambda e, uh=uh, fg=fg: e.tensor_tensor(out=uh[:], in0=Dp[3][:, 0:2 * NFC].rearrange("p (c t) -> p c t", t=2),
                                                                        in1=fg[:].unsqueeze(1).to_broadcast([128, NFC, 2]), op=ALU.mult),
                         reads=[bD[3][0], bfg], writes=[buh])
                    def stageA(j):
                        outs = []
                        for half, c in enumerate((j, NPAIR + j)):
                            pi = pic[0]
                            pic[0] += 1
                            pu = Dp[pi % 3][:, (pi // 3 % 2) * 512:(pi // 3 % 2) * 512 + 512]
                            bpu = bD[pi % 3][pi // 3 % 2]
                            U, bU = U_r.get()
                            S.group("pe", [lambda e, c=c, k=k, pu=pu: e.matmul(pu[:, 0:gw], lhsT=wup[:, k, c * 128:(c + 1) * 128],
                                                                              rhs=h2win[:, k, 1:gw + 1], start=(k == 0), stop=(k == 7))
                                           for k in range(8)], reads=[bh2w, bwup], writes=[bpu])
                            acc, bacc = acc_r.get()
                            S.op("act", lambda e, U=U, pu=pu: e.activation(out=U[:, 1:gw + 1], in_=pu[:, 0:gw], func=AF.Copy), reads=[bpu], writes=[bU])
                            S.op("act", lambda e, acc=acc, pu=pu, c=c: e.activation(out=acc[:, 0:gw], in_=pu[:, 0:gw], func=AF.Identity,
                                                                                    scale=cw[:, c, 1:2], bias=cw[:, c, 3:4]),
                                 reads=[bpu, bcw], writes=[bacc])
                            S.op("pool", lambda e, U=U, c=c: e.tensor_copy(U[:, 0:gw + 2:gw + 1], uh[:, c, :]), reads=[buh, bU], writes=[bU])
                            outs.append((U, bU, c, acc, bacc))
                        return outs

                    def stageB(j, outs):
                        accs = []
                        for half, (U, bU, c, acc, bacc) in enumerate(outs):
                            S.op("dve", lambda e, U=U, acc=acc, c=c: e.scalar_tensor_tensor(out=acc[:, 0:gw], in0=U[:, 0:gw], scalar=cw[:, c, 0:1],
                                                                                           in1=acc[:, 0:gw], op0=ALU.mult, op1=ALU.add),
                                 reads=[bU, bcw, bacc], writes=[bacc])
                            S.op("dve", lambda e, U=U, acc=acc, c=c: e.scalar_tensor_tensor(out=acc[:, 0:gw], in0=U[:, 2:gw + 2], scalar=cw[:, c, 2:3],
                                                                                           in1=acc[:, 0:gw], op0=ALU.mult, op1=ALU.add),
                                 reads=[bU, bcw, bacc], writes=[bacc])
                            accs.append((acc, bacc))
                        (aa, baa), (ag, bag) = accs
                        S.op("act", lambda e: e.activation(out=ag[:, 0:gw], in_=ag[:, 0:gw], func=AF.Silu), reads=[bag], writes=[bag])
                        S.op("pool", lambda e: e.tensor_tensor(out=actT[:, j, 0:gw], in0=aa[:, 0:gw], in1=ag[:, 0:gw], op=ALU.mult),
                             reads=[baa, bag], writes=[bact])

                    pend = {}
                    for j in range(NPAIR + 1):
                        recs = []
                        if j < NPAIR:
                            recs.append(S.record(lambda: pend.__setitem__(j, stageA(j))))
                        if j >= 1:
                            o_ = pend.pop(j - 1)
                            recs.append(S.record(lambda: stageB(j - 1, o_)))
                        S.interleave(recs)
                    for tl in range(ntl):
                        gi = main_base[si] + t0 + tl
                        x1, bx1 = x1_r.get(); sq, bsq = sq_r.get(); ss, bss = ss_r.get()
                        y, by = x1, bx1
                        S.dma(x1[:], x1_d[gi], bx1, reads=[b_x1d[gi]], writes=[bx1])
                        dq = tl % 2
                        for cb in range(2):
                            S.group("pe", [lambda e, j=j, cb=cb, tl=tl, dq=dq: e.matmul(Dp[dq][:, cb * 512:(cb + 1) * 512], lhsT=actT[:, j, tl * 128:(tl + 1) * 128],
                                                                                       rhs=wdn[:, j, cb * 512:(cb + 1) * 512], start=(j == 0), stop=(j == NPAIR - 1))
                                           for j in range(NPAIR)], reads=[bact, bwdn], writes=[bD[dq][cb]])
                        S.op("dve", lambda e, y=y, x1=x1, dq=dq: e.tensor_tensor(out=y[:], in0=Dp[dq][:], in1=x1[:], op=ALU.add),
                             reads=bD[dq] + [bx1], writes=[bx1])
                        S.op("act", lambda e, sq=sq, y=y, ss=ss: e.activation(out=sq[:], in_=y[:], func=AF.Square, accum_out=ss[:, 0:1]),
                             reads=[by], writes=[bsq, bss])
                        rsqrt_mean(ss[:, 1:2], ss[:, 0:1], D, [bss], [bss])
                        S.op("dve", lambda e, y=y, ss=ss: e.scalar_tensor_tensor(out=y[:], in0=y[:], scalar=ss[:, 1:2], in1=wfin[:], op0=ALU.mult, op1=ALU.mult),
                             reads=[by, bss, b_wfin], writes=[by])
                        S.dma(y_d[gi], y[:], by, reads=[by], writes=[b_y])
                for g in range(ngr):
                    ffn_group(g)
            S.flush()
    return nc


def rope_tables(pos):
    pos = np.asarray(pos)
    row = (pos // 64).astype(np.float32)
    col = (pos % 64).astype(np.float32)
    nf = HD // 4
    inv = (np.float32(10000.0) ** (-np.arange(nf, dtype=np.float32) / np.float32(nf))).astype(np.float32)
    ar = row[:, None] * inv
    ac = col[:, None] * inv
    ang = np.concatenate([ar, ar, ac, ac], axis=-1).astype(np.float32)
    cos = np.cos(ang).astype(np.float32)
    sin = np.sin(ang).astype(np.float32)
    sgn = np.concatenate([-np.ones(16), np.ones(16), -np.ones(16), np.ones(16)]).astype(np.float32)
    return np.concatenate([cos, sin * sgn], axis=-1).astype(np.float32)


def prep_inputs(cfg, x_prompt, x_sample, w_in, b_gates, mh_norm_w, q_norm_w, k_norm_w, w_out, norm1_w, norm2_w,
                w_up, conv_w, conv_b, w_down, final_norm_w):
    f32 = np.float32
    xp = np.asarray(x_prompt, f32).reshape(-1, 128, D)
    xsm = np.asarray(x_sample, f32).reshape(x_sample.shape[0], -1, 128, D)
    NP, SEG, NCTX = cfg.NP, cfg.SEG, cfg.NCTX
    hperm = [0, 4, 1, 5, 2, 6, 3, 7]
    w_in0 = np.asarray(w_in, f32)[0]
    aq = w_in0[:, C_AQ:C_AQ + 512].reshape(D, 8, 64)[:, hperm, :].reshape(D, 512)
    w_in_p = np.concatenate([w_in0[:, :C_AQ], aq, w_in0[:, C_AQ + 512:]], axis=1)
    w_out0 = np.asarray(w_out, f32)[0]
    wa = w_out0[512:].reshape(8, 64, D)[hperm].reshape(512, D)
    w_out_p = np.concatenate([w_out0[:512], wa], axis=0)
    rep = lambda v: np.ascontiguousarray(np.broadcast_to(np.asarray(v, f32).reshape(1, -1), (128, np.asarray(v).size)))
    pk = lambda v: np.ascontiguousarray(np.asarray(v, f32).reshape(8, 128).T)
    cwv = np.asarray(conv_w, f32)[0]
    cbv = np.asarray(conv_b, f32)[0]
    convpk = np.stack([cwv[0], cwv[1], cwv[2], cbv], axis=-1).reshape(NFC, 128, 4).transpose(1, 0, 2).reshape(128, NFC * 4)
    wqk = np.concatenate([np.tile(np.asarray(q_norm_w, f32)[0], 8), np.tile(np.asarray(k_norm_w, f32)[0], 2)])
    common = {
        "w_in": np.ascontiguousarray(w_in_p), "w_out": np.ascontiguousarray(w_out_p),
        "w_up": np.ascontiguousarray(np.asarray(w_up, f32)[0]), "w_down": np.ascontiguousarray(np.asarray(w_down, f32)[0]),
        "n1pk": pk(np.asarray(norm1_w)[0]), "n2pk": pk(np.asarray(norm2_w)[0]), "convpk": np.ascontiguousarray(convpk),
        "bg_bc": rep(np.asarray(b_gates)[0]), "mhw_bc": rep(np.asarray(mh_norm_w)[0]), "wqk_bc": rep(wqk),
        "wfin_bc": rep(final_norm_w),
    }
    samp_pos = rope_tables(np.arange(cfg.ST * 128)).reshape(cfg.ST, 128, 128)
    zt = np.zeros((128, D), f32)
    in_maps = []
    for c in range(cfg.NC):
        tiles, cs = [], []
        for i in range(cfg.NSAMP):
            b = c * cfg.NSAMP + i
            tiles += [xsm[b, t] for t in range(cfg.ST)]
            cs += [samp_pos[t] for t in range(cfg.ST)]
        lo, hi = c * SEG, (c + 1) * SEG
        ptiles = list(range(lo, hi)) + [lo - 1, hi]
        for t in ptiles:
            ok = 0 <= t < NP
            tiles.append(xp[t] if ok else zt)
            cs.append(rope_tables(np.arange(t * 128, (t + 1) * 128) if ok else np.zeros(128, np.int64)))
        left = list(range(0, max(lo - 1, 0)))
        right = list(range(NP - 1, hi, -1))
        ctx = [(t, 1.0, 0.0) for t in left] + [(t, 0.0, 1.0) for t in right]
        while len(ctx) < max(NCTX, 1):
            ctx.append((None, 0.0, 0.0))
        xc = np.stack([xp[t] if t is not None else zt for t, _, _ in ctx])
        csc = np.stack([rope_tables(np.arange(t * 128, (t + 1) * 128) if t is not None else np.zeros(128, np.int64)) for t, _, _ in ctx])
        fl = np.array([[l, r, 0.0 if t is None else 1.0] for t, l, r in ctx], f32).reshape(1, -1)
        m = dict(common)
        m.update({
            "xs": np.ascontiguousarray(np.stack(tiles)), "xc": np.ascontiguousarray(xc),
            "cs_own": np.ascontiguousarray(np.stack(cs)), "cs_ctx": np.ascontiguousarray(csc),
            "fl_ctx": np.ascontiguousarray(np.broadcast_to(fl, (128, fl.shape[1]))),
            "fl_own": np.ascontiguousarray(np.broadcast_to(np.array([[float(lo > 0), float(hi < NP), 0, 0]], f32), (128, 4))),
        })
        in_maps.append(m)
    return in_maps


def assemble(cfg, results, nb_samp):
    yp = np.zeros((cfg.NP, 128, D), np.float32)
    ys = np.zeros((nb_samp, cfg.ST, 128, D), np.float32)
    for c in range(cfg.NC):
        y = np.asarray(results[c]["y"])
        o = 0
        for i in range(cfg.NSAMP):
            ys[c * cfg.NSAMP + i] = y[o:o + cfg.ST]
            o += cfg.ST
        yp[c * cfg.SEG:(c + 1) * cfg.SEG] = y[o:o + cfg.SEG]
    return yp.reshape(1, cfg.NP * 128, D), ys.reshape(nb_samp, cfg.ST * 128, D)


_CACHE = {}


def run(cfg, **inputs):
    key = (cfg.NC, cfg.SEG, cfg.ST, cfg.NSAMP, cfg.GT)
    if key not in _CACHE:
        _CACHE[key] = build_program(cfg)
    nc = _CACHE[key]
    in_maps = prep_inputs(cfg, **inputs)
    res = run_bass_kernel_spmd(nc, in_maps, core_ids=list(range(cfg.NC)))
    return assemble(cfg, res.results, inputs["x_sample"].shape[0])


def kernel(**inputs):
    cfg = Cfg()
    return run(cfg, **inputs)
```

```python
import os
import numpy as np
from contextlib import ExitStack
import concourse.bass as bass
import concourse.mybir as mybir
from concourse.bass_utils import run_bass_kernel_spmd

F32 = mybir.dt.float32
BF16 = mybir.dt.bfloat16
AF = mybir.ActivationFunctionType
ALU = mybir.AluOpType
AX = mybir.AxisListType

D = 1024
HD = 64
MW = 512
AW = 512
KVW = 128
NG = 32
INC = 2848
DFF = 2816
NFC = 44
NPAIR = 22
EPS = 1e-6
C_MQ, C_MK, C_MV, C_MO, C_G, C_AQ, C_AK, C_AV = 0, 512, 1024, 1536, 2048, 2080, 2592, 2720


class Buf:
    __slots__ = ("name", "last_w", "readers", "dsem", "dcnt", "excl")

    def __init__(self, name, excl=False):
        self.name = name
        self.excl = excl
        self.last_w = None
        self.readers = []
        self.dsem = None
        self.dcnt = 0


class Sched:
    ENGS = ("pe", "act", "dve", "pool", "sp")

    def __init__(self, nc, stack):
        self.nc = nc
        self.stack = stack
        self.q = {e: [] for e in self.ENGS}
        self.sem = {}
        self.cnt = {}
        for e in ("pe", "act", "dve", "pool"):
            self.sem[e] = stack.enter_context(nc.semaphore("prog_" + e))
            self.cnt[e] = 0
        self.known = {e: {} for e in self.ENGS}
        self.dma_best = {}
        self.semobjs = {}
        self.ninst = 0
        self.free_dsems = []
        self.phase_owners = []
        self._rec = None
        self._rec_ret = None
        self.nflush = 0
        self.skip = set(int(x) for x in os.environ.get("KDEBUG_SKIP", "").split(",") if x)
        self.trace = bool(os.environ.get("KDEBUG_TRACE"))
        self.max_flush = int(os.environ.get("KDEBUG_STOP", "1000000"))
        self.max_ops = int(os.environ.get("KDEBUG_OPS", "100000000"))

    def _deps(self, eng, reads, writes):
        deps = {}

        def add(tok):
            if tok is None:
                return
            s, v = tok
            k = id(s)
            self.semobjs[k] = s
            if deps.get(k, 0) < v:
                deps[k] = v
        for b in reads:
            add(b.last_w)
            if b.excl:
                for r in b.readers:
                    add(r)
        for b in writes:
            add(b.last_w)
            for r in b.readers:
                add(r)
        out = []
        kn = self.known[eng]
        for k, v in deps.items():
            if eng == "pe" and self.semobjs[k] is self.sem["pe"]:
                continue
            if kn.get(k, 0) >= v:
                continue
            kn[k] = v
            out.append((self.semobjs[k], v))
        return out

    def _commit(self, tok, reads, writes):
        for b in writes:
            b.last_w = tok
            b.readers = []
        for b in reads:
            if b not in writes:
                if b.excl:
                    b.readers = [tok]
                    continue
                b.readers.append(tok)
                if len(b.readers) > 48:
                    best = {}
                    for s, v in b.readers:
                        if best.get(id(s), (None, 0))[1] < v:
                            best[id(s)] = (s, v)
                    b.readers = list(best.values())

    def _trace(self, eng):
        if self.trace:
            import traceback
            fr = traceback.extract_stack(limit=4)[0]
            print("OP", self.ninst, eng, fr.lineno, fr.line[:90], flush=True)

    def record(self, fn, *args):
        assert self._rec is None
        self._rec = []
        self._rec_ret = fn(*args)
        rec, self._rec = self._rec, None
        return rec

    def interleave(self, recs):
        recs = [list(r) for r in recs if r]
        idx = [0] * len(recs)
        live = True
        while live:
            live = False
            for k, r in enumerate(recs):
                if idx[k] < len(r):
                    kind, a = r[idx[k]]
                    idx[k] += 1
                    live = True
                    getattr(self, kind)(*a)

    def interleave_spread(self, main, side):
        n, m = len(main), len(side)
        j = 0
        for i, (kind, a) in enumerate(main):
            while j < m and j * n <= i * m:
                k2, a2 = side[j]
                j += 1
                getattr(self, k2)(*a2)
            getattr(self, kind)(*a)
        while j < m:
            k2, a2 = side[j]
            j += 1
            getattr(self, k2)(*a2)

    def dead(self):
        if self.ninst in self.skip:
            self.ninst += 1
            print("SKIPPED op", self.ninst - 1, flush=True)
            return True
        return self.nflush >= self.max_flush or self.ninst >= self.max_ops

    def op(self, eng, fn, reads=(), writes=()):
        if self._rec is not None:
            self._rec.append(("op", (eng, fn, list(reads), list(writes))))
            return None
        if self.dead():
            return None
        self._trace(eng)
        waits = self._deps(eng, reads, writes)
        self.cnt[eng] += 1
        tok = (self.sem[eng], self.cnt[eng])
        self.q[eng].append((waits, fn, tok, False))
        self._commit(tok, reads, writes)
        self.ninst += 1
        return tok

    def group(self, eng, fns, reads=(), writes=()):
        if self._rec is not None:
            self._rec.append(("group", (eng, fns, list(reads), list(writes))))
            return None
        if self.dead():
            return None
        self._trace(eng)
        waits = self._deps(eng, reads, writes)
        self.cnt[eng] += 1
        tok = (self.sem[eng], self.cnt[eng])
        n = len(fns)
        for i, fn in enumerate(fns):
            self.q[eng].append((waits if i == 0 else [], fn, tok if i == n - 1 else None, False))
        self._commit(tok, reads, writes)
        self.ninst += n
        return tok

    def dma(self, out_ap, in_ap, owner, reads=(), writes=(), q="sp"):
        if self._rec is not None:
            self._rec.append(("dma", (out_ap, in_ap, owner, list(reads), list(writes), q)))
            return None
        if self.dead():
            return None
        if owner.dsem is None:
            if self.free_dsems and q == "sp":
                owner.dsem, owner.dcnt = self.free_dsems.pop()
                self.phase_owners.append(owner)
            else:
                owner.dsem = self.stack.enter_context(self.nc.semaphore(_uniq("d_" + owner.name)))
                owner.dcnt = 0
                if q == "sp":
                    self.phase_owners.append(owner)
        self._trace("dma")
        waits = self._deps(q, reads, writes)
        owner.dcnt += 16
        tok = (owner.dsem, owner.dcnt)

        def fn(e, out_ap=out_ap, in_ap=in_ap):
            return e.dma_start(out=out_ap, in_=in_ap)
        self.q[q].append((waits, fn, tok, True))
        self._commit(tok, reads, writes)
        self.dma_best[id(owner.dsem)] = tok
        self.ninst += 1
        return tok

    def flush(self):
        nc = self.nc
        self.nflush += 1
        bar = [(self.sem[e], self.cnt[e]) for e in ("pe", "act", "dve", "pool") if self.cnt[e] > 0]
        bar += list(self.dma_best.values())
        qs = self.q
        self.q = {e: [] for e in self.ENGS}
        known = self.known

        def run(engname, e):
            for waits, fn, tok, isdma in qs[engname]:
                for s, v in waits:
                    e.wait_ge(s, v)
                ins = fn(e)
                if tok is not None:
                    ins.then_inc(tok[0], 16 if isdma else 1)
            for s, v in bar:
                if known[engname].get(id(s), 0) < v:
                    e.wait_ge(s, v)
                    known[engname][id(s)] = v

        with nc.allow_low_precision(reason="bf16 matmul operands by design"), nc.allow_non_contiguous_dma(reason="single halo columns"), nc.Block() as blk:
            @blk.tensor
            def _(e):
                run("pe", e)

            @blk.scalar
            def _(e):
                run("act", e)

            @blk.vector
            def _(e):
                run("dve", e)

            @blk.gpsimd
            def _(e):
                run("pool", e)

            @blk.sync
            def _(e):
                run("sp", e)
        for o in self.phase_owners:
            if o.dsem is not None:
                self.free_dsems.append((o.dsem, o.dcnt))
                o.dsem = None
        self.phase_owners = []


_UID = [0]
_HEADS = [int(x) for x in os.environ.get('KDEBUG_HEADS', '0,2,4,6,1,3,5,7').split(',')]


def _uniq(name):
    _UID[0] += 1
    return f"{name}_{_UID[0]}"


class Ring:
    def __init__(self, nc, st, name, shape, dt, n):
        self.items = []
        for i in range(n):
            t = st.enter_context(nc.sbuf_tensor(_uniq(f"{name}{i}"), shape, dt))
            self.items.append((t, Buf(f"{name}{i}")))
        self.i = 0

    def get(self):
        it = self.items[self.i % len(self.items)]
        self.i += 1
        return it


class Cfg:
    def __init__(self, ncores=8, seg=16, samp_tiles=16, nsamp=2, gt=4):
        self.NC = ncores
        self.SEG = seg
        self.ST = samp_tiles
        self.NSAMP = nsamp
        self.GT = gt
        self.NP = ncores * seg
        self.NCTX = self.NP - seg - 1
        self.segs = [("s%d" % i, samp_tiles, False) for i in range(nsamp)] + [("p", seg, True)]
        self.own_tiles = sum(m + (2 if h else 0) for _, m, h in self.segs)
        self.main_tiles = sum(m for _, m, h in self.segs)


def build_program(cfg):
    nc = bass.Bass("TRN2", target_bir_lowering=False)
    NOWN, NMAIN, NCTX = cfg.own_tiles, cfg.main_tiles, cfg.NCTX
    NTOK = NOWN * 128

    def din(name, shape, dt=F32):
        return nc.dram_tensor(name, shape, dt, kind="ExternalInput").ap()

    xs_d = din("xs", [NOWN, 128, D])
    xc_d = din("xc", [max(NCTX, 1), 128, D])
    cso_d = din("cs_own", [NOWN, 128, 128])
    csc_d = din("cs_ctx", [max(NCTX, 1), 128, 128])
    flc_d = din("fl_ctx", [128, max(NCTX, 1) * 3])
    flo_d = din("fl_own", [128, 4])
    w_in_d = din("w_in", [D, INC])
    w_out_d = din("w_out", [D, D])
    w_up_d = din("w_up", [D, 2 * DFF])
    w_down_d = din("w_down", [DFF, D])
    n1_d = din("n1pk", [128, 8])
    n2_d = din("n2pk", [128, 8])
    cw_d = din("convpk", [128, NFC * 4])
    bg_d = din("bg_bc", [128, NG])
    mhw_d = din("mhw_bc", [128, MW])
    wqk_d = din("wqk_bc", [128, 640])
    wfin_d = din("wfin_bc", [128, D])
    y_d = nc.dram_tensor("y", [NMAIN, 128, D], F32, kind="ExternalOutput").ap()
    mixT_d = nc.dram_tensor("mixT_s", [D, NTOK], BF16, kind="Internal").ap()
    x1_d = nc.dram_tensor("x1_s", [NMAIN, 128, D], F32, kind="Internal").ap()
    h2w = [m * 128 + 2 for _, m, _ in cfg.segs]
    h2T_d = [nc.dram_tensor(f"h2T_s{i}", [D, h2w[i]], BF16, kind="Internal").ap() for i in range(len(cfg.segs))]
    recd_d = nc.dram_tensor("recd_s", [4, 512], F32, kind="Internal").ap()
    b_recd = [Buf(f"recd{i}") for i in range(4)]
    w_in_s = nc.dram_tensor("w_in_bf", [D, INC], BF16, kind="Internal").ap()
    w_out_s = nc.dram_tensor("w_out_bf", [D, D], BF16, kind="Internal").ap()
    w_up_s = nc.dram_tensor("w_up_bf", [D, 2 * DFF], BF16, kind="Internal").ap()
    w_dn_s = nc.dram_tensor("w_dn_bf", [DFF, D], BF16, kind="Internal").ap()
    b_wins, b_wouts, b_wups, b_wdns = Buf("w_in_s"), Buf("w_out_s"), Buf("w_up_s"), Buf("w_dn_s")
    b_mixT = Buf("mixT_d")
    b_x1d = [Buf(f"x1d{i}") for i in range(NMAIN)]
    b_h2d = [Buf(f"h2d{i}") for i in range(len(cfg.segs))]
    b_y = Buf("y_d")

    with ExitStack() as gst:
        S = Sched(nc, gst)
        sb = lambda st, name, shape, dt: st.enter_context(nc.sbuf_tensor(_uniq(name), shape, dt))
        ps = lambda st, name, shape, dt: st.enter_context(nc.psum_tensor(_uniq(name), shape, dt))

        ident = sb(gst, "ident", [128, 128], BF16); b_ident = Buf("ident")
        idf = sb(gst, "idf", [128, 128], F32); b_idf = Buf("idf")
        trif = sb(gst, "trif", [128, 128], F32); b_trif = Buf("trif")
        onesf = sb(gst, "onesf", [128, 128], F32); b_onesf = Buf("onesf")
        maskf = sb(gst, "maskf", [128, 128], F32); b_maskf = Buf("maskf")
        maskb = sb(gst, "maskb", [128, 128], F32); b_maskb = Buf("maskb")
        epst = sb(gst, "epst", [128, 1], F32); b_eps = Buf("eps")
        onet = sb(gst, "onet", [128, 1], F32); b_one = Buf("one")
        bg = sb(gst, "bg", [128, NG], F32); b_bg = Buf("bg")
        mhw = sb(gst, "mhw", [128, MW], F32); b_mhw = Buf("mhw")
        wqk = sb(gst, "wqk", [128, 640], F32); b_wqk = Buf("wqk")
        wfin = sb(gst, "wfin", [128, D], F32); b_wfin = Buf("wfin")
        n1 = sb(gst, "n1", [128, 8], F32); b_n1 = Buf("n1")
        n2 = sb(gst, "n2", [128, 8], F32); b_n2 = Buf("n2")
        flo = sb(gst, "flo", [128, 4], F32); b_flo = Buf("flo")
        negc = sb(gst, "negc", [128, 1], F32); b_negc = Buf("negc")
        mx2 = sb(gst, "mx2", [128, 2], F32); b_mx2 = Buf("mx2")
        cinit = sb(gst, "cinit", [128, 2, 8, 65], F32); b_cinit = Buf("cinit")

        for t, b, v in ((idf, b_idf, 1.0), (trif, b_trif, 1.0), (onesf, b_onesf, 1.0), (epst, b_eps, EPS),
                        (onet, b_one, 1.0)):
            S.op("pool", lambda e, t=t, v=v: e.memset(t[:], v), writes=[b])
        S.op("pool", lambda e: e.affine_select(out=idf[:], in_=idf[:], pattern=[[-1, 128]], compare_op=ALU.is_equal,
                                               fill=0.0, base=0, channel_multiplier=1), reads=[b_idf], writes=[b_idf])
        S.op("pool", lambda e: e.affine_select(out=trif[:], in_=trif[:], pattern=[[1, 128]], compare_op=ALU.is_ge,
                                               fill=0.0, base=0, channel_multiplier=-1), reads=[b_trif], writes=[b_trif])
        S.op("dve", lambda e: e.tensor_copy(ident[:], idf[:]), reads=[b_idf], writes=[b_ident])
        S.op("dve", lambda e: e.tensor_copy(maskf[:], trif[:]), reads=[b_trif], writes=[b_maskf])
        S.op("pool", lambda e: e.memset(maskb[:], 1.0), writes=[b_maskb])
        S.op("pool", lambda e: e.affine_select(out=maskb[:], in_=maskb[:], pattern=[[-1, 128]], compare_op=ALU.is_ge,
                                               fill=0.0, base=0, channel_multiplier=1), reads=[b_maskb], writes=[b_maskb])
        for t, b, d in ((bg, b_bg, bg_d), (mhw, b_mhw, mhw_d), (wqk, b_wqk, wqk_d), (wfin, b_wfin, wfin_d),
                        (n1, b_n1, n1_d), (n2, b_n2, n2_d), (flo, b_flo, flo_d)):
            S.dma(t[:], d, b, writes=[b])
        S.op("dve", lambda e: e.tensor_scalar(out=wqk[:, 0:512], in0=wqk[:, 0:512], scalar1=0.125, scalar2=None,
                                              op0=ALU.mult), reads=[b_wqk], writes=[b_wqk])
        S.op("dve", lambda e: e.tensor_reduce(out=mx2[:, 0:1], in_=wqk[:, 0:64], axis=AX.X, op=ALU.max,
                                              apply_absolute_value=True), reads=[b_wqk], writes=[b_mx2])
        S.op("dve", lambda e: e.tensor_reduce(out=mx2[:, 1:2], in_=wqk[:, 512:576], axis=AX.X, op=ALU.max,
                                              apply_absolute_value=True), reads=[b_wqk, b_mx2], writes=[b_mx2])
        S.op("dve", lambda e: e.tensor_scalar(out=negc[:], in0=mx2[:, 0:1], scalar1=mx2[:, 1:2], scalar2=-64.0,
                                              op0=ALU.mult, op1=ALU.mult), reads=[b_mx2], writes=[b_negc])
        S.op("pool", lambda e: e.memset(cinit[:], 0.0), writes=[b_cinit])
        S.flush()

        def make_conv(stw, dmaq):
            stg_r = Ring(nc, stw, "cvs", [128, 2848], F32, 3)
            stb_r = Ring(nc, stw, "cvb", [128, 2848], BF16, 3)

            def conv_rows(src_ap, dst_ap, ncol, scale_ap, b_dst, eighth=None):
                stg, bstg = stg_r.get(); stb, bstb = stb_r.get()
                S.dma(stg[:, 0:ncol], src_ap, bstg, writes=[bstg], q=dmaq)
                if scale_ap is None:
                    S.op("dve", lambda e: e.tensor_copy(stb[:, 0:ncol], stg[:, 0:ncol]), reads=[bstg], writes=[bstb])
                else:
                    S.op("dve", lambda e: e.tensor_scalar(out=stb[:, 0:ncol], in0=stg[:, 0:ncol], scalar1=scale_ap, scalar2=None,
                                                          op0=ALU.mult), reads=[bstg, b_n1, b_n2], writes=[bstb])
                if eighth is not None:
                    a, b = eighth
                    S.op("dve", lambda e: e.tensor_scalar(out=stb[:, a:b], in0=stb[:, a:b], scalar1=0.125, scalar2=None,
                                                          op0=ALU.mult), reads=[bstb], writes=[bstb])
                S.dma(dst_ap, stb[:, 0:ncol], bstb, reads=[bstb], writes=[b_dst], q=dmaq)
            return conv_rows

        def conv_ffn_weights(conv_rows):
            for k in range(8):
                conv_rows(w_out_d[k * 128:(k + 1) * 128, :], w_out_s[k * 128:(k + 1) * 128, :], D, None, b_wouts)
            for k in range(8):
                for hh in range(2):
                    conv_rows(w_up_d[k * 128:(k + 1) * 128, hh * DFF:(hh + 1) * DFF], w_up_s[k * 128:(k + 1) * 128, hh * DFF:(hh + 1) * DFF],
                              DFF, n2[:, k:k + 1], b_wups)
            for j in range(NPAIR):
                conv_rows(w_down_d[j * 128:(j + 1) * 128, :], w_dn_s[j * 128:(j + 1) * 128, :], D, None, b_wdns)

        with ExitStack() as stw:
            conv_a = make_conv(stw, "sp")
            conv_b = make_conv(stw, "pool")
            for k in range(8):
                (conv_a if k % 2 == 0 else conv_b)(w_in_d[k * 128:(k + 1) * 128, :], w_in_s[k * 128:(k + 1) * 128, :], INC,
                                                   n1[:, k:k + 1], b_wins, eighth=(C_MK, C_MK + 512))
            S.flush()

        def load_w_bf(st, name, src_s, b_src, c0, c1):
            w = sb(st, name, [128, 8, c1 - c0], BF16)
            bw = Buf(name)
            S.dma(w[:], src_s[:, c0:c1].rearrange("(k p) c -> p k c", p=128), bw, reads=[b_src], writes=[bw])
            return w, bw

        def rsqrt_mean(dst, src, n, rd, wr):
            S.op("act", lambda e: e.activation(out=dst, in_=src, func=AF.Ln, scale=1.0 / n, bias=epst[:]),
                 reads=rd + [b_eps], writes=wr)
            S.op("act", lambda e: e.activation(out=dst, in_=dst, func=AF.Exp, scale=-0.5), reads=wr, writes=wr)

        class TileCtx:
            def __init__(self, st, pT, nb=2):
                self.xt = Ring(nc, st, "xt", [128, D], F32, nb)
                self.nb = nb
                self.sq = Ring(nc, st, "sqj", [128, D], BF16, 1)
                self.ss = Ring(nc, st, "ssq", [128, 2], F32, 2)
                self.hn = Ring(nc, st, "hn", [128, D], BF16, nb)
                self.hT = Ring(nc, st, "hT", [128, 8, 128], BF16, 2)
                self.pT = pT
                self.pi = 0

            def next_pT(self):
                it = self.pT[self.pi % len(self.pT)]
                self.pi += 1
                return it

            def norm_T_from_sbuf(self, xt, bx):
                sq, bsq = self.sq.get()
                ss, bss = self.ss.get()
                hn, bhn = self.hn.get()
                hT, bhT = self.hT.get()
                pT, bpT = self.next_pT()
                S.op("act", lambda e: e.activation(out=sq[:], in_=xt[:], func=AF.Square, accum_out=ss[:, 0:1]),
                     reads=[bx], writes=[bsq, bss])
                rsqrt_mean(ss[:, 1:2], ss[:, 0:1], D, [bss], [bss])
                S.op("dve", lambda e: e.tensor_scalar(out=hn[:], in0=xt[:], scalar1=ss[:, 1:2], scalar2=None,
                                                      op0=ALU.mult), reads=[bx, bss], writes=[bhn])
                S.group("pe", [lambda e, k=k: e.transpose(pT[:, k * 128:(k + 1) * 128], hn[:, k * 128:(k + 1) * 128],
                                                          ident[:]) for k in range(8)],
                        reads=[bhn, b_ident], writes=[bpT])
                self.pi2 = getattr(self, "pi2", 0) + 1
                if self.pi2 % 4 != 3:
                    S.op("act", lambda e: e.activation(out=hT[:].rearrange("p a b -> p (a b)"), in_=pT[:], func=AF.Copy),
                         reads=[bpT], writes=[bhT])
                else:
                    S.op("dve", lambda e: e.tensor_copy(hT[:].rearrange("p a b -> p (a b)"), pT[:]), reads=[bpT], writes=[bhT])
                return hT, bhT

            def norm_T(self, src_ap):
                xt, bx = self.xt.get()
                S.dma(xt[:], src_ap, bx, writes=[bx])
                hT, bhT = self.norm_T_from_sbuf(xt, bx)
                return hT, bhT, xt, bx

        def load_w_cols(st, name, src_d, c0, c1, scale_pk, b_scale, stage):
            w = sb(st, name, [128, 8, c1 - c0], BF16)
            bw = Buf(name)
            for k in range(8):
                stg, bstg = stage.get()
                S.dma(stg[:, 0:c1 - c0], src_d[k * 128:(k + 1) * 128, c0:c1], bstg, writes=[bstg])
                eng = "dve" if k % 2 == 0 else "pool"
                S.op(eng, lambda e, k=k, stg=stg: e.tensor_scalar(out=w[:, k, :], in0=stg[:, 0:c1 - c0],
                                                                   scalar1=scale_pk[:, k:k + 1], scalar2=None, op0=ALU.mult),
                     reads=[bstg, b_scale], writes=[bw])
            return w, bw

        def proj(hT, bhT, w, bw, c0, c1, pso, bpso):
            S.group("pe", [lambda e, k=k: e.matmul(pso[:, 0:c1 - c0], lhsT=hT[:, k, :], rhs=w[:, k, c0:c1],
                                                   start=(k == 0), stop=(k == 7)) for k in range(8)],
                    reads=[bhT, bw], writes=[bpso])

        seg_base = []
        main_base = []
        o = 0
        mo_ = 0
        for _, m, h in cfg.segs:
            seg_base.append(o)
            main_base.append(mo_)
            o += m + (2 if h else 0)
            mo_ += m

        def gate_vectors(st_name, G, bG, lf, blf, pcs, bpcs, vec, bvec, flags=None):
            S.op("act", lambda e: e.activation(out=lf[:, 0:16], in_=G[:, 16:32], func=AF.Exp, scale=-1.0), reads=[bG], writes=[blf])
            S.op("act", lambda e: e.activation(out=lf[:, 0:16], in_=lf[:, 0:16], func=AF.Ln, scale=1.0, bias=onet[:]),
                 reads=[blf, b_one], writes=[blf])
            S.op("dve", lambda e: e.tensor_scalar(out=lf[:, 0:16], in0=lf[:, 0:16], scalar1=-1.0, scalar2=None, op0=ALU.mult),
                 reads=[blf], writes=[blf])
            S.group("pe", [lambda e: e.matmul(pcs[:, 0:16], lhsT=trif[:], rhs=lf[:, 0:16], start=True, stop=True),
                           lambda e: e.matmul(pcs[:, 16:32], lhsT=onesf[:], rhs=lf[:, 0:16], start=True, stop=True)],
                    reads=[blf, b_trif, b_onesf], writes=[bpcs])
            S.op("dve", lambda e: e.tensor_copy(lf[:, 16:24], pcs[:, 0:8]), reads=[bpcs, blf], writes=[blf])
            S.op("dve", lambda e: e.tensor_tensor(out=lf[:, 24:32], in0=pcs[:, 24:32], in1=lf[:, 8:16], op=ALU.add),
                 reads=[bpcs, blf], writes=[blf])
            S.op("dve", lambda e: e.tensor_tensor(out=lf[:, 24:32], in0=lf[:, 24:32], in1=pcs[:, 8:16], op=ALU.subtract),
                 reads=[bpcs, blf], writes=[blf])
            S.op("act", lambda e: e.activation(out=vec[:, 0, :], in_=lf[:, 16:32], func=AF.Exp), reads=[blf], writes=[bvec])
            S.op("dve", lambda e: e.tensor_tensor(out=lf[:, 16:32], in0=G[:, 0:16], in1=lf[:, 16:32], op=ALU.subtract),
                 reads=[bG, blf], writes=[blf])
            S.op("act", lambda e: e.activation(out=vec[:, 1, :], in_=lf[:, 16:32], func=AF.Exp), reads=[blf], writes=[bvec])
            if flags is None:
                S.op("act", lambda e: e.activation(out=vec[:, 2, :], in_=pcs[:, 16:32], func=AF.Exp), reads=[bpcs], writes=[bvec])
            else:
                for d_, fl in enumerate(flags):
                    S.op("act", lambda e, d_=d_, fl=fl: e.activation(out=vec[:, 2, d_ * 8:(d_ + 1) * 8], in_=pcs[:, 16 + d_ * 8:24 + d_ * 8],
                                                                     func=AF.Exp, scale=fl), reads=[bpcs, b_flc], writes=[bvec])
            S.op("act", lambda e: e.activation(out=vec[:, 3, :], in_=pcs[:, 16:32], func=AF.Exp), reads=[bpcs], writes=[bvec])
            S.op("dve", lambda e: e.tensor_tensor(out=vec[:, 3, :], in0=vec[:, 3, :], in1=vec[:, 1, :], op=ALU.mult),
                 reads=[bvec], writes=[bvec])
            if flags is not None:
                for d_, fl in enumerate(flags):
                    S.op("dve", lambda e, d_=d_, fl=fl: e.tensor_scalar(out=vec[:, 3, d_ * 8:(d_ + 1) * 8], in0=vec[:, 3, d_ * 8:(d_ + 1) * 8],
                                                                        scalar1=fl, scalar2=None, op0=ALU.mult),
                         reads=[bvec, b_flc], writes=[bvec])

        b_flc = Buf("flc")

        def qk_norm_rope(qk, bqk, nh, wsl, cs, bcs, tmp, btmp, tmp2, btmp2, rs, brs, outb, boutb, eng="dve"):
            v3 = lambda ap: ap.rearrange("p (h d) -> p h d", d=64)
            S.op(eng, lambda e: e.tensor_tensor(out=tmp[:, 0:nh * 64], in0=qk[:, 0:nh * 64], in1=qk[:, 0:nh * 64], op=ALU.mult),
                 reads=[bqk], writes=[btmp])
            S.op("dve", lambda e: e.tensor_reduce(out=rs[:, 0:nh], in_=v3(tmp[:, 0:nh * 64]), axis=AX.X, op=ALU.add),
                 reads=[btmp], writes=[brs])
            rsqrt_mean(rs[:, 0:nh], rs[:, 0:nh], HD, [brs], [brs])
            S.op(eng, lambda e: e.tensor_tensor(out=v3(qk[:, 0:nh * 64]), in0=v3(qk[:, 0:nh * 64]),
                                                in1=rs[:, 0:nh].unsqueeze(2).to_broadcast([128, nh, 64]), op=ALU.mult),
                 reads=[bqk, brs], writes=[bqk])
            S.op(eng, lambda e: e.tensor_tensor(out=qk[:, 0:nh * 64], in0=qk[:, 0:nh * 64], in1=wsl, op=ALU.mult),
                 reads=[bqk, b_wqk], writes=[bqk])
            v5 = lambda ap: ap.rearrange("p (h a t s) -> p h a t s", a=2, t=2, s=16)
            c4 = cs[:, 0:64].rearrange("p (a t s) -> p a t s", a=2, t=2, s=16)
            s4 = cs[:, 64:128].rearrange("p (a t s) -> p a t s", a=2, t=2, s=16)
            for t_ in range(2):
                S.op(eng, lambda e, t_=t_: e.tensor_tensor(
                    out=v5(tmp[:, 0:nh * 64])[:, :, :, t_, :], in0=v5(qk[:, 0:nh * 64])[:, :, :, 1 - t_, :],
                    in1=s4[:, :, t_, :].unsqueeze(1).to_broadcast([128, nh, 2, 16]), op=ALU.mult),
                    reads=[bqk, bcs, btmp], writes=[btmp])
            S.op(eng, lambda e: e.tensor_tensor(out=v3(tmp2[:, 0:nh * 64]), in0=v3(qk[:, 0:nh * 64]),
                                                in1=cs[:, 0:64].unsqueeze(1).to_broadcast([128, nh, 64]), op=ALU.mult),
                 reads=[bqk, bcs], writes=[btmp2])
            S.op(eng, lambda e: e.tensor_tensor(out=outb[:, 0:nh * 64], in0=tmp2[:, 0:nh * 64], in1=tmp[:, 0:nh * 64], op=ALU.add),
                 reads=[btmp, btmp2], writes=[boutb])

        for si, (sname, M, halo) in enumerate(cfg.segs):
            NT = M + (2 if halo else 0)
            base = seg_base[si]
            tok0 = base * 128
            nctx = NCTX if halo else 0
            NK = (nctx + NT)
            with ExitStack() as st:
                kT = sb(st, "kT", [128, NK * 128], BF16); bkT = Buf("kT")
                vA = sb(st, "vA", [128, NK, 2, 65], BF16); bvA = Buf("vA")
                aqT = sb(st, "aqT", [128, 4, 2, NT * 128], BF16); baqT = Buf("aqT")
                S.op("pool", lambda e: e.memset(aqT[:], 0.0), writes=[baqT])
                st_outer = st
                st = ExitStack()
                pT = [(ps(st, f"pT{i}", [128, 1024], BF16), Buf(f"pT{i}", True)) for i in range(2)]
                P_kv = ps(st, "Pkv", [128, 512], F32); bPkv = Buf("Pkv", True)
                P_a = ps(st, "Pa", [128, 512], F32); bPa = Buf("Pa", True)
                P_b = ps(st, "Pb", [128, 512], F32); bPb = Buf("Pb", True)
                P_g = ps(st, "Pg", [128, 512], F32); bPg = Buf("Pg", True)
                P_D = ps(st, "PD", [128, 1024], F32); bPD = [Buf("PDlo", True), Buf("PDhi", True)]
                tc = TileCtx(st, pT)
                wa, bwa = load_w_bf(st, "wa", w_in_s, b_wins, C_AQ, INC)
                qkr = Ring(nc, st, "qk", [128, 640], F32, 3)
                tmpr = Ring(nc, st, "qtmp", [128, 640], F32, 2)
                tmp2r = Ring(nc, st, "qtmp2", [128, 640], F32, 2)
                rsr = Ring(nc, st, "qrs", [128, 16], F32, 2)
                qbr = Ring(nc, st, "qb", [128, 640], BF16, 2)
                csr = Ring(nc, st, "cs", [128, 128], F32, 3)
                if halo and nctx > 0:
                    wm, bwm = load_w_bf(st, "wm", w_in_s, b_wins, C_MK, C_MK + 1024)
                    wg, bwg = load_w_bf(st, "wg", w_in_s, b_wins, C_G, C_G + NG)
                    flc = sb(st, "flc", [128, nctx, 3], F32)
                    nflc = sb(st, "nflc", [128, nctx, 3], F32)
                    S.dma(flc[:].rearrange("p a b -> p (a b)"), flc_d, b_flc, writes=[b_flc])
                    S.op("dve", lambda e: e.tensor_scalar(out=nflc[:], in0=flc[:], scalar1=-1.0, scalar2=None, op0=ALU.mult),
                         reads=[b_flc], writes=[b_flc])
                    Gr = Ring(nc, st, "cG", [128, NG], F32, 3)
                    lfr = Ring(nc, st, "clf", [128, 32], F32, 2)
                    vecr = Ring(nc, st, "cvec", [128, 4, 16], F32, 2)
                    mkr = Ring(nc, st, "cmk", [128, 512], BF16, 3)
                    vhr = Ring(nc, st, "cvh", [128, 8, 65], BF16, 2)
                    vfr = Ring(nc, st, "cvf", [128, 8, 65], F32, 3)
                    for vf_, bvf_ in vfr.items:
                        S.op("pool", lambda e, vf_=vf_: e.memset(vf_[:, :, 64:65], 1.0), writes=[bvf_])
                    Rst = sb(st, "Rst", [128, 2, 8, 65], F32); bR = Buf("Rst")
                    S.op("pool", lambda e: e.memset(Rst[:], 0.0), writes=[bR])

                def att_A(pre, cs_ap, kslot, qslot, valid_ap):
                    hT, bhT, xt, bx = pre
                    cs, bcs = csr.get()
                    S.dma(cs[:], cs_ap, bcs, writes=[bcs])
                    qk, bqk = qkr.get()
                    if qslot is not None:
                        proj(hT, bhT, wa, bwa, 0, 512, P_a[:, 0:512], bPa)
                        S.op("act", lambda e: e.activation(out=qk[:, 0:512], in_=P_a[:, 0:512], func=AF.Copy),
                             reads=[bPa], writes=[bqk])
                    proj(hT, bhT, wa, bwa, 512, 768, P_kv[:, 0:256], bPkv)
                    ko = 512 if qslot is not None else 0
                    S.op("act", lambda e: e.activation(out=qk[:, ko:ko + 128], in_=P_kv[:, 0:128], func=AF.Copy),
                         reads=[bPkv], writes=[bqk])
                    S.op("act", lambda e: e.activation(out=vA[:, kslot, :, 0:64],
                                                       in_=P_kv[:, 128:256].rearrange("p (h d) -> p h d", d=64), func=AF.Copy),
                         reads=[bPkv], writes=[bvA])
                    if valid_ap is None:
                        S.op("pool", lambda e: e.memset(vA[:, kslot, :, 64:65], 1.0), writes=[bvA])
                    else:
                        S.op("pool", lambda e: e.tensor_copy(vA[:, kslot, :, 64:65], valid_ap.unsqueeze(1).to_broadcast([128, 2, 1])),
                             reads=[b_flc, b_flo], writes=[bvA])
                    return (qk, bqk, cs, bcs, kslot, qslot)

                def att_B(hA):
                    qk, bqk, cs, bcs, kslot, qslot = hA
                    tmp, btmp = tmpr.get(); tmp2, btmp2 = tmp2r.get()
                    rs, brs = rsr.get(); qb, bqb = qbr.get()
                    if qslot is not None:
                        qk_norm_rope(qk, bqk, 10, wqk[:, 0:640], cs, bcs, tmp, btmp, tmp2, btmp2, rs, brs, qb, bqb,
                                     eng="dve" if kslot % 2 == 0 else "pool")
                        pT_, bpT_ = tc.next_pT()
                        S.group("pe", [lambda e, j=j: e.transpose(pT_[:, j * 128:(j + 1) * 128], qb[:, j * 128:(j + 1) * 128], ident[:])
                                       for j in range(5)], reads=[bqb, b_ident], writes=[bpT_])
                        S.op("act", lambda e: e.activation(out=aqT[0:64, :, 0, qslot * 128:(qslot + 1) * 128],
                                                           in_=pT_[0:64, 0:512].rearrange("p (j t) -> p j t", t=128), func=AF.Copy),
                             reads=[bpT_], writes=[baqT])
                        S.op("act", lambda e: e.activation(out=aqT[64:128, :, 1, qslot * 128:(qslot + 1) * 128],
                                                           in_=pT_[64:128, 0:512].rearrange("p (j t) -> p j t", t=128), func=AF.Copy),
                             reads=[bpT_], writes=[baqT])
                        S.op("dve", lambda e: e.tensor_copy(kT[:, kslot * 128:(kslot + 1) * 128], pT_[:, 512:640]),
                             reads=[bpT_], writes=[bkT])
                    else:
                        qk_norm_rope(qk, bqk, 2, wqk[:, 512:640], cs, bcs, tmp, btmp, tmp2, btmp2, rs, brs, qb, bqb, eng="pool")
                        pT_, bpT_ = tc.next_pT()
                        S.group("pe", [lambda e: e.transpose(pT_[:, 0:128], qb[:, 0:128], ident[:])], reads=[bqb, b_ident], writes=[bpT_])
                        S.op("act", lambda e: e.activation(out=kT[:, kslot * 128:(kslot + 1) * 128], in_=pT_[:, 0:128], func=AF.Copy),
                             reads=[bpT_], writes=[bkT])

                def ctx_A(ci, pre):
                    hT, bhT, xt, bx = pre
                    G, bG = Gr.get(); mk, bmk = mkr.get(); vf, bvf = vfr.get()
                    proj(hT, bhT, wm, bwm, 0, 512, P_a[:, 0:512], bPa)
                    S.op("act", lambda e: e.activation(out=mk[:], in_=P_a[:, 0:512], func=AF.Copy), reads=[bPa], writes=[bmk])
                    proj(hT, bhT, wm, bwm, 512, 1024, P_b[:, 0:512], bPb)
                    S.op("act", lambda e: e.activation(out=vf[:, :, 0:64], in_=P_b[:, 0:512].rearrange("p (h d) -> p h d", d=64),
                                                       func=AF.Copy), reads=[bPb], writes=[bvf])
                    proj(hT, bhT, wg, bwg, 0, NG, P_g[:, 0:NG], bPg)
                    S.op("dve", lambda e: e.tensor_tensor(out=G[:], in0=P_g[:, 0:NG], in1=bg[:], op=ALU.add),
                         reads=[bPg, b_bg], writes=[bG])
                    return (ci, G, bG, mk, bmk, vf, bvf)

                def ctx_B(hC):
                    ci, G, bG, mk, bmk, vf, bvf = hC
                    lf, blf = lfr.get(); vec, bvec = vecr.get(); vh, bvh = vhr.get()
                    fl = (flc[:, ci, 0:1], flc[:, ci, 1:2])
                    nfl = (nflc[:, ci, 0:1], nflc[:, ci, 1:2])
                    pcs = P_g[:, 64:96]
                    S.op("act", lambda e: e.activation(out=lf[:, 0:16], in_=G[:, 16:32], func=AF.Exp, scale=-1.0), reads=[bG], writes=[blf])
                    S.op("act", lambda e: e.activation(out=lf[:, 0:16], in_=lf[:, 0:16], func=AF.Ln, scale=1.0, bias=onet[:]),
                         reads=[blf, b_one], writes=[blf])
                    S.group("pe", [lambda e: e.matmul(pcs[:, 0:16], lhsT=trif[:], rhs=lf[:, 0:16], start=True, stop=True),
                                   lambda e: e.matmul(pcs[:, 16:32], lhsT=onesf[:], rhs=lf[:, 0:16], start=True, stop=True)],
                            reads=[blf, b_trif, b_onesf], writes=[bPg])
                    S.op("dve", lambda e: e.tensor_tensor(out=vec[:, 0, 0:8], in0=G[:, 0:8], in1=pcs[:, 0:8], op=ALU.add), reads=[bG, bPg], writes=[bvec])
                    S.op("dve", lambda e: e.tensor_tensor(out=vec[:, 0, 0:8], in0=vec[:, 0, 0:8], in1=pcs[:, 16:24], op=ALU.subtract),
                         reads=[bvec, bPg], writes=[bvec])
                    S.op("dve", lambda e: e.tensor_tensor(out=vec[:, 0, 8:16], in0=G[:, 8:16], in1=lf[:, 8:16], op=ALU.add), reads=[bG, blf], writes=[bvec])
                    S.op("dve", lambda e: e.tensor_tensor(out=vec[:, 0, 8:16], in0=vec[:, 0, 8:16], in1=pcs[:, 8:16], op=ALU.subtract),
                         reads=[bvec, bPg], writes=[bvec])
                    S.op("act", lambda e: e.activation(out=vec[:, 1, :], in_=vec[:, 0, :], func=AF.Exp), reads=[bvec], writes=[bvec])
                    for d_ in range(2):
                        S.op("act", lambda e, d_=d_: e.activation(out=vec[:, 2, d_ * 8:(d_ + 1) * 8], in_=pcs[:, 16 + d_ * 8:24 + d_ * 8],
                                                                  func=AF.Exp, scale=nfl[d_]), reads=[bPg, b_flc], writes=[bvec])
                    S.op("dve", lambda e: e.tensor_scalar(out=lf[:, 16:24], in0=vec[:, 1, 0:8], scalar1=fl[0], scalar2=None, op0=ALU.mult),
                         reads=[bvec, b_flc, blf], writes=[blf])
                    S.op("dve", lambda e: e.scalar_tensor_tensor(out=lf[:, 0:8], in0=vec[:, 1, 8:16], scalar=fl[1], in1=lf[:, 16:24],
                                                                 op0=ALU.mult, op1=ALU.add), reads=[bvec, b_flc, blf], writes=[blf])
                    S.op("pool", lambda e: e.tensor_tensor(out=vh[:], in0=vf[:], in1=lf[:, 0:8].unsqueeze(2).to_broadcast([128, 8, 65]),
                                                           op=ALU.mult), reads=[bvf, blf], writes=[bvh])
                    S.group("pe", [lambda e, pr=pr: e.matmul(
                        P_D[:, (pr // 2) * 512 + (pr % 2) * 130:(pr // 2) * 512 + (pr % 2) * 130 + 130],
                        lhsT=mk[:, pr * 128:(pr + 1) * 128],
                        rhs=vh[:, 2 * pr:2 * pr + 2, :].rearrange("p h c -> p (h c)"), start=True, stop=True)
                        for pr in range(4)], reads=[bmk, bvh], writes=bPD)
                    pc4 = P_D[:].rearrange("p (b c) -> p b c", b=2)[:, :, 0:260]
                    for d_ in range(2):
                        Rv = Rst[:, d_].rearrange("p h c -> p (h c)").rearrange("p (b c) -> p b c", b=2)
                        S.op("dve", lambda e, d_=d_: e.tensor_tensor(
                            out=Rst[:, d_], in0=Rst[:, d_], in1=vec[:, 2, d_ * 8:(d_ + 1) * 8].unsqueeze(2).to_broadcast([128, 8, 65]),
                            op=ALU.mult), reads=[bR, bvec], writes=[bR])
                        S.op("dve", lambda e, Rv=Rv, d_=d_: e.scalar_tensor_tensor(out=Rv, in0=pc4, scalar=fl[d_], in1=Rv,
                                                                                   op0=ALU.mult, op1=ALU.add),
                             reads=[bR, b_flc] + bPD, writes=[bR])
                    if ci == nctx - 1:
                        S.op("dve", lambda e: e.tensor_copy(cinit[:], Rst[:]), reads=[bR], writes=[b_cinit])

                srcs = [xc_d[ci] for ci in range(nctx)] + [xs_d[base + t_] for t_ in range(NT)]
                ntile = len(srcs)
                pre, hA, hC = {}, {}, {}
                for i in range(ntile + 2):
                    recs = []
                    if i < ntile:
                        recs.append(S.record(lambda: pre.__setitem__(i, tc.norm_T(srcs[i]))))
                    k_ = i - 1
                    if 0 <= k_ < ntile:
                        p_ = pre.pop(k_)
                        if k_ < nctx:
                            recs.append(S.record(lambda: hA.__setitem__(k_, att_A(p_, csc_d[k_], k_, None, flc[:, k_, 2:3]))))
                            recs.append(S.record(lambda: hC.__setitem__(k_, ctx_A(k_, p_))))
                        else:
                            t_ = k_ - nctx
                            valid = None
                            if halo and t_ >= M:
                                valid = flo[:, (t_ - M):(t_ - M) + 1]
                            recs.append(S.record(lambda: hA.__setitem__(k_, att_A(p_, cso_d[base + t_], nctx + t_, t_, valid))))
                    k2 = i - 2
                    if 0 <= k2 < ntile:
                        recs.append(S.record(lambda: att_B(hA.pop(k2))))
                        if k2 < nctx:
                            recs.append(S.record(lambda: ctx_B(hC.pop(k2))))
                    S.interleave(recs)

                S.flush()
                st.close()
                st = st_outer
                NSB = 3 if NK > 32 else 2
                NOB = 2 if NK > 32 else 4
                Sps = [(ps(st, f"S{i}", [128, 1024], F32), Buf(f"S{i}", True)) for i in range(NSB)]
                Obanks = [(ps(st, f"O{i}", [128, 512], F32), Buf(f"O{i}", True)) for i in range(NOB)]
                PTr = Ring(nc, st, "PT", [128, 2, 512], BF16, 4)
                recr = Ring(nc, st, "rec", [65, 512], F32, 4)
                bcsr = Ring(nc, st, "bcs", [64, 512], F32, 4)
                onr = Ring(nc, st, "on", [64, 512], BF16, 4)
                groups = []
                q0 = 0
                while q0 < M * 128:
                    gq = min(cfg.GT * 128, M * 128 - q0)
                    groups.append((q0, gq))
                    q0 += gq
                if halo:
                    groups.append((M * 128 + 127, 2))
                    zt_ = sb(st, "zhalo", [128, 256], BF16); bzt_ = Buf("zhalo")
                    S.op("pool", lambda e: e.memset(zt_[:], 0.0), writes=[bzt_])
                    for r_ in range(4):
                        S.dma(mixT_d[512 + r_ * 128:512 + (r_ + 1) * 128, tok0 + M * 128:tok0 + M * 128 + 256], zt_[:], bzt_,
                              reads=[bzt_], writes=[b_mixT])
                units = [(q0, gq, j, half) for (q0, gq) in groups for j in range(4) for half in range(2)]
                work = [(ui, kt, min(2, NK - kt)) for ui in range(len(units)) for kt in range(0, NK, 2)]
                pts = {}

                def emit_scores(wi):
                    ui, kt0, nk = work[wi]
                    q0, gq, j, half = units[ui]
                    hs = slice(64 * half, 64 * half + 64)
                    pS, bpS = Sps[wi % NSB]
                    PT, bPT = PTr.get()
                    pts[wi] = (PT, bPT)
                    S.group("pe", [lambda e, i=i: e.matmul(pS[:, i * 512:i * 512 + gq], lhsT=kT[:, (kt0 + i) * 128:(kt0 + i + 1) * 128],
                                                           rhs=aqT[:, j, half, q0:q0 + gq], start=True, stop=True) for i in range(nk)],
                            reads=[bkT, baqT], writes=[bpS])
                    S.op("act", lambda e: e.activation(out=PT[:, 0:nk, 0:gq], in_=pS[:].rearrange("p (a b) -> p a b", a=2)[:, 0:nk, 0:gq],
                                                       func=AF.Exp, bias=negc[:], scale=1.0),
                         reads=[bpS, b_negc], writes=[bPT])

                def emit_pv(wi):
                    ui, kt0, nk = work[wi]
                    q0, gq, j, half = units[ui]
                    po, bpo = Obanks[ui % NOB]
                    PT, bPT = pts.pop(wi)
                    S.group("pe", [lambda e, i=i: e.matmul(po[0:65, 0:gq], lhsT=vA[:, kt0 + i, half, :], rhs=PT[:, i, 0:gq],
                                                           start=(kt0 + i == 0), stop=(kt0 + i == NK - 1)) for i in range(nk)],
                            reads=[bvA, bPT], writes=[bpo])
                    if kt0 + nk == NK:
                        rec, brec = recr.get(); bcs_, bbcs = bcsr.get(); on, bon = onr.get()
                        S.op("dve", lambda e: e.reciprocal(rec[64:65, 0:gq], po[64:65, 0:gq]), reads=[bpo], writes=[brec])
                        slot = ui % 4
                        S.dma(recd_d[slot:slot + 1, 0:gq], rec[64:65, 0:gq], b_recd[slot], reads=[brec], writes=[b_recd[slot]])
                        S.dma(bcs_[:, 0:gq], recd_d[slot:slot + 1, 0:gq].to_broadcast([64, gq]), bbcs,
                              reads=[b_recd[slot]], writes=[bbcs])
                        S.op("dve", lambda e: e.tensor_tensor(out=on[:, 0:gq], in0=po[0:64, 0:gq], in1=bcs_[:, 0:gq],
                                                              op=ALU.mult), reads=[bpo, bbcs], writes=[bon])
                        r0 = 512 + (2 * j + half) * 64
                        S.dma(mixT_d[r0:r0 + 64, tok0 + q0:tok0 + q0 + gq], on[:, 0:gq], bon, reads=[bon], writes=[b_mixT])

                LOOK = NSB - 1

                def att_stream():
                    for step in range(len(work) + LOOK):
                        if step < len(work):
                            emit_scores(step)
                        if step - LOOK >= 0:
                            emit_pv(step - LOOK)
                if si == 0:
                    conv_bg = make_conv(st, "pool")
                    S.interleave_spread(S.record(att_stream), S.record(lambda: conv_ffn_weights(conv_bg)))
                else:
                    att_stream()
                S.flush()

            with ExitStack() as st:
                pT = [(ps(st, f"pT{i}", [128, 1024], BF16), Buf(f"pT{i}", True)) for i in range(2)]
                Dp = [ps(st, f"D{i}", [128, 1024], F32) for i in range(3)]
                bD = [[Buf(f"D{i}lo", True), Buf(f"D{i}hi", True)] for i in range(3)]
                qT = sb(st, "mqT", [128, 4, NT * 128], BF16); bqT = Buf("mqT")
                kTm = sb(st, "mkT", [128, 4, NT * 128], BF16); bkTm = Buf("mkT")
                mkA = sb(st, "mkA", [128, NT, 512], BF16); bmkA = Buf("mkA")
                mvA = sb(st, "mvA", [128, NT, 8, 65], BF16); bmvA = Buf("mvA")
                sgA = sb(st, "sgA", [128, NT, 512], BF16); bsgA = Buf("sgA")
                vecA = sb(st, "vecA", [128, NT, 4, 16], F32)
                bvecA = [Buf(f"vecA{t_}") for t_ in range(NT)]
                hfA = sb(st, "hfA", [128, NT, 512], BF16); bhfA = [Buf(f"hfA{t_}") for t_ in range(NT)]
                maskfb = sb(st, "maskfb", [128, 128], BF16); maskbb = sb(st, "maskbb", [128, 128], BF16)
                st_outer = st
                st = ExitStack()
                tc = TileCtx(st, pT, nb=1)
                wmm, bwmm = load_w_bf(st, "wmm", w_in_s, b_wins, 0, 2080)
                mqb_r = Ring(nc, st, "mqb", [128, 1024], BF16, 1)
                G_all = sb(st, "mGall", [128, NT, NG], F32); bGall = Buf("mGall")
                lf_all = sb(st, "mlfall", [128, NT, 32], F32); blfall = Buf("mlfall")
                sg_r = Ring(nc, st, "sgt", [128, 512], F32, 1)
                S.op("pool", lambda e: e.memset(mvA[:, :, :, 64:65], 1.0), writes=[bmvA])
                def mproj_tile(t_, pre):
                    hT, bhT, xt, bx = pre
                    mqb, bmqb = mqb_r.get()
                    proj(hT, bhT, wmm, bwmm, 0, 512, Dp[0][:, 0:512], bD[0][0])
                    S.op("act", lambda e, mqb=mqb: e.activation(out=mqb[:, 0:512], in_=Dp[0][:, 0:512], func=AF.Copy), reads=[bD[0][0]], writes=[bmqb])
                    proj(hT, bhT, wmm, bwmm, 512, 1024, Dp[0][:, 512:1024], bD[0][1])
                    S.op("act", lambda e, mqb=mqb: e.activation(out=mqb[:, 512:1024], in_=Dp[0][:, 512:1024], func=AF.Copy), reads=[bD[0][1]], writes=[bmqb])
                    S.op("pool", lambda e, mqb=mqb, t_=t_: e.tensor_copy(mkA[:, t_, :], mqb[:, 512:1024]), reads=[bmqb], writes=[bmkA])
                    pT_, bpT_ = tc.next_pT()
                    S.group("pe", [lambda e, j=j, mqb=mqb, pT_=pT_: e.transpose(pT_[:, j * 128:(j + 1) * 128], mqb[:, j * 128:(j + 1) * 128], ident[:])
                                   for j in range(8)], reads=[bmqb, b_ident], writes=[bpT_])
                    S.op("act", lambda e, pT_=pT_, t_=t_: e.activation(out=qT[:, :, t_ * 128:(t_ + 1) * 128],
                                                                      in_=pT_[:, 0:512].rearrange("p (j t) -> p j t", t=128), func=AF.Copy),
                         reads=[bpT_], writes=[bqT])
                    S.op("dve", lambda e, pT_=pT_, t_=t_: e.tensor_copy(kTm[:, :, t_ * 128:(t_ + 1) * 128],
                                                                       pT_[:, 512:1024].rearrange("p (j t) -> p j t", t=128)),
                         reads=[bpT_], writes=[bkTm])

                def mproj_tile2(t_, pre):
                    hT, bhT, xt, bx = pre
                    sgt, bsgt = sg_r.get()
                    proj(hT, bhT, wmm, bwmm, 1024, 1536, Dp[1][:, 0:512], bD[1][0])
                    S.op("act", lambda e, t_=t_: e.activation(out=mvA[:, t_, :, 0:64], in_=Dp[1][:, 0:512].rearrange("p (h d) -> p h d", d=64),
                                                              func=AF.Copy), reads=[bD[1][0]], writes=[bmvA])
                    proj(hT, bhT, wmm, bwmm, 1536, 2048, Dp[1][:, 512:1024], bD[1][1])
                    S.op("act", lambda e, sgt=sgt: e.activation(out=sgt[:], in_=Dp[1][:, 512:1024], func=AF.Exp, scale=-1.0),
                         reads=[bD[1][1]], writes=[bsgt])
                    S.op("pool", lambda e, sgt=sgt: e.tensor_scalar(out=sgt[:], in0=sgt[:], scalar1=1.0, scalar2=None, op0=ALU.add),
                         reads=[bsgt], writes=[bsgt])
                    S.op("dve", lambda e, sgt=sgt, t_=t_: e.reciprocal(sgA[:, t_, :], sgt[:]), reads=[bsgt], writes=[bsgA])
                    proj(hT, bhT, wmm, bwmm, 2048, 2080, Dp[2][:, 0:NG], bD[2][0])
                    S.op("dve", lambda e: e.tensor_tensor(out=G_all[:, t_, :], in0=Dp[2][:, 0:NG], in1=bg[:], op=ALU.add),
                         reads=[bD[2][0], b_bg], writes=[bGall])
                pre = {}
                for i in range(NT + 1):
                    recs = []
                    if i < NT:
                        recs.append(S.record(lambda: pre.__setitem__(i, tc.norm_T(xs_d[base + i]))))
                    if i >= 1:
                        p_ = pre.pop(i - 1)
                        recs.append(S.record(lambda: mproj_tile(i - 1, p_)))
                        recs.append(S.record(lambda: mproj_tile2(i - 1, p_)))
                    S.interleave(recs)
                G3, lf3 = G_all, lf_all
                S.op("act", lambda e: e.activation(out=lf3[:, :, 0:16], in_=G3[:, :, 16:32], func=AF.Exp, scale=-1.0), reads=[bGall], writes=[blfall])
                S.op("act", lambda e: e.activation(out=lf3[:, :, 0:16], in_=lf3[:, :, 0:16], func=AF.Ln, scale=1.0, bias=onet[:]),
                     reads=[blfall, b_one], writes=[blfall])
                S.op("dve", lambda e: e.tensor_scalar(out=lf3[:, :, 0:16], in0=lf3[:, :, 0:16], scalar1=-1.0, scalar2=None, op0=ALU.mult),
                     reads=[blfall], writes=[blfall])
                fns = []
                for t_ in range(NT):
                    fns.append(lambda e, t_=t_: e.matmul(Dp[2][:, t_ * 32:t_ * 32 + 16], lhsT=trif[:], rhs=lf3[:, t_, 0:16], start=True, stop=True))
                    fns.append(lambda e, t_=t_: e.matmul(Dp[2][:, t_ * 32 + 16:t_ * 32 + 32], lhsT=onesf[:], rhs=lf3[:, t_, 0:16], start=True, stop=True))
                S.group("pe", fns, reads=[blfall, b_trif, b_onesf], writes=bD[2])
                pcs3 = Dp[2][:, 0:NT * 32].rearrange("p (t c) -> p t c", c=32)
                S.op("dve", lambda e: e.tensor_copy(lf3[:, :, 16:24], pcs3[:, :, 0:8]), reads=bD[2] + [blfall], writes=[blfall])
                S.op("dve", lambda e: e.tensor_tensor(out=lf3[:, :, 24:32], in0=pcs3[:, :, 24:32], in1=lf3[:, :, 8:16], op=ALU.add),
                     reads=bD[2] + [blfall], writes=[blfall])
                S.op("dve", lambda e: e.tensor_tensor(out=lf3[:, :, 24:32], in0=lf3[:, :, 24:32], in1=pcs3[:, :, 8:16], op=ALU.subtract),
                     reads=bD[2] + [blfall], writes=[blfall])
                S.op("act", lambda e: e.activation(out=vecA[:, :, 0, :], in_=lf3[:, :, 16:32], func=AF.Exp), reads=[blfall], writes=bvecA)
                S.op("dve", lambda e: e.tensor_tensor(out=lf3[:, :, 16:32], in0=G3[:, :, 0:16], in1=lf3[:, :, 16:32], op=ALU.subtract),
                     reads=[bGall, blfall], writes=[blfall])
                S.op("act", lambda e: e.activation(out=vecA[:, :, 1, :], in_=lf3[:, :, 16:32], func=AF.Exp), reads=[blfall], writes=bvecA)
                S.op("act", lambda e: e.activation(out=vecA[:, :, 2, :], in_=pcs3[:, :, 16:32], func=AF.Exp), reads=bD[2], writes=bvecA)
                S.op("dve", lambda e: e.tensor_tensor(out=vecA[:, :, 3, :], in0=vecA[:, :, 2, :], in1=vecA[:, :, 1, :], op=ALU.mult),
                     reads=bvecA, writes=bvecA)
                S.op("dve", lambda e: e.tensor_copy(maskfb[:], maskf[:]), reads=[b_maskf], writes=[b_maskf])
                S.op("dve", lambda e: e.tensor_copy(maskbb[:], maskb[:]), reads=[b_maskb], writes=[b_maskb])

                S.flush()
                st.close()
                st = st_outer
                pTi = [0]

                def next_pT():
                    it = pT[pTi[0] % 2]
                    pTi[0] += 1
                    return it
                Cst = sb(st, "Cst", [128, 8, 65], F32); bC = Buf("Cst")
                Cbf_r = Ring(nc, st, "Cbf", [128, 8, 65], BF16, 3)
                cbh = {}
                bmask = sb(st, "bmask", [128, 4, 2, 65], F32); bbm = Buf("bmask")
                S.op("pool", lambda e: e.memset(bmask[:], 0.0), writes=[bbm])
                S.op("pool", lambda e: e.memset(bmask[0:64, :, 0, :], 1.0), reads=[bbm], writes=[bbm])
                S.op("pool", lambda e: e.memset(bmask[64:128, :, 1, :], 1.0), reads=[bbm], writes=[bbm])
                bmask3 = bmask[:].rearrange("p a b c -> p (a b) c")
                PTm_r = Ring(nc, st, "PTm", [128, 8, 128], BF16, 2)
                vt_r = Ring(nc, st, "vt", [128, 8, 65], BF16, 2)
                vh_r = Ring(nc, st, "vhm", [128, 8, 65], BF16, 2)
                r_r = Ring(nc, st, "rr", [128, 24], F32, 2)
                hs_r = Ring(nc, st, "hs", [128, 512], F32, 2)
                h2_r = Ring(nc, st, "hsq", [128, 512], F32, 2)
                rs_r = Ring(nc, st, "mrs", [128, 8], F32, 2)
                mo_r = Ring(nc, st, "mout", [128, 512], BF16, 2)
                mT_r = Ring(nc, st, "mTt", [128, 4, 128], BF16, 2)
                seq = ([M] if halo else []) + list(range(M)) + ([M + 1] if halo else [])
                for d_ in (1, 0):
                    order = seq if d_ == 0 else seq[::-1]
                    mask = maskfb if d_ == 0 else maskbb
                    if halo:
                        S.op("dve", lambda e, d_=d_: e.tensor_copy(Cst[:], cinit[:, d_]), reads=[b_cinit], writes=[bC])
                    else:
                        S.op("pool", lambda e: e.memset(Cst[:], 0.0), writes=[bC])
                    Cbf0, bCbf0 = Cbf_r.get()
                    S.op("pool", lambda e, Cbf0=Cbf0: e.tensor_tensor(out=Cbf0[:], in0=Cst[:], in1=bmask3, op=ALU.mult), reads=[bC, bbm], writes=[bCbf0])
                    cbh["cur"] = (Cbf0, bCbf0)
                    def scan_S(t_, d_=d_):
                        vsl = lambda k_: vecA[:, t_, k_, d_ * 8:(d_ + 1) * 8]
                        vh, bvh = vh_r.get()
                        S.op("pool", lambda e, vh=vh, t_=t_, vsl=vsl: e.tensor_tensor(
                            out=vh[:], in0=mvA[:, t_], in1=vsl(3).unsqueeze(2).to_broadcast([128, 8, 65]), op=ALU.mult),
                            reads=[bmvA, bvecA[t_]], writes=[bvh])
                        S.group("pe", [lambda e, pr=pr, vh=vh, t_=t_: e.matmul(
                            Dp[2][:, (pr // 2) * 512 + (pr % 2) * 130:(pr // 2) * 512 + (pr % 2) * 130 + 130],
                            lhsT=mkA[:, t_, pr * 128:(pr + 1) * 128],
                            rhs=vh[:, 2 * pr:2 * pr + 2, :].rearrange("p h c -> p (h c)"), start=True, stop=True)
                            for pr in range(4)], reads=[bmkA, bvh], writes=bD[2])
                        S.op("dve", lambda e, vsl=vsl: e.tensor_tensor(out=Cst[:], in0=Cst[:], in1=vsl(2).unsqueeze(2).to_broadcast([128, 8, 65]),
                                                                       op=ALU.mult), reads=[bC, bvecA[t_]], writes=[bC])
                        Cv = Cst[:].rearrange("p h c -> p (h c)").rearrange("p (b c) -> p b c", b=2)
                        pc4 = Dp[2][:].rearrange("p (b c) -> p b c", b=2)[:, :, 0:260]
                        S.op("dve", lambda e, Cv=Cv, pc4=pc4: e.tensor_tensor(out=Cv, in0=Cv, in1=pc4, op=ALU.add), reads=[bC] + bD[2], writes=[bC])
                        Cbn, bCbn = Cbf_r.get()
                        S.op("pool", lambda e: e.tensor_tensor(out=Cbn[:], in0=Cst[:], in1=bmask3, op=ALU.mult), reads=[bC, bbm], writes=[bCbn])
                        return (Cbn, bCbn)

                    def scan_H(t_, cprev, d_=d_, mask=mask):
                        Cbf, bCbf = cprev
                        ts = slice(t_ * 128, (t_ + 1) * 128)
                        vsl = lambda k_: vecA[:, t_, k_, d_ * 8:(d_ + 1) * 8]
                        PTm, bPTm = PTm_r.get(); vt, bvt = vt_r.get(); rr, brr = r_r.get()
                        S.op("pool", lambda e, vt=vt, t_=t_, vsl=vsl: e.tensor_tensor(
                            out=vt[:], in0=mvA[:, t_], in1=vsl(1).unsqueeze(2).to_broadcast([128, 8, 65]), op=ALU.mult),
                            reads=[bmvA, bvecA[t_]], writes=[bvt])
                        S.group("pe", [lambda e, h=h, ts=ts: e.matmul(Dp[0][:, ((h % 2) * 4 + h // 2) * 128:((h % 2) * 4 + h // 2 + 1) * 128],
                                                                     lhsT=kTm[64 * (h % 2):64 * (h % 2) + 64, h // 2, ts],
                                                                     rhs=qT[64 * (h % 2):64 * (h % 2) + 64, h // 2, ts], start=True, stop=True)
                                       for h in range(8)], reads=[bkTm, bqT], writes=bD[0])
                        S.op("dve", lambda e, PTm=PTm, mask=mask: e.tensor_tensor(
                            out=PTm[:], in0=Dp[0][:].rearrange("p (h l) -> p h l", h=8),
                            in1=mask[:].unsqueeze(1).to_broadcast([128, 8, 128]), op=ALU.mult),
                            reads=bD[0] + [b_maskf, b_maskb], writes=[bPTm])
                        fns = []
                        for h in range(8):
                            c0 = (h // 4) * 512 + (h % 4) * 65
                            hp = slice(64 * (h % 2), 64 * (h % 2) + 64)
                            fns.append(lambda e, h=h, c0=c0, PTm=PTm, vt=vt: e.matmul(Dp[1][:, c0:c0 + 65], lhsT=PTm[:, (h % 2) * 4 + h // 2, :], rhs=vt[:, h, :],
                                                                                     start=True, stop=False))
                            fns.append(lambda e, h=h, c0=c0, ts=ts: e.matmul(Dp[1][:, c0:c0 + 65], lhsT=qT[:, h // 2, ts],
                                                                            rhs=Cbf[:, h, :], start=False, stop=True))
                        S.group("pe", fns, reads=[bPTm, bvt, bqT, bCbf], writes=bD[1])
                        nd = Dp[1][:].rearrange("p (b c) -> p b c", b=2)[:, :, 0:260].rearrange("p b (g c) -> p b g c", c=65)
                        a8 = vsl(0).rearrange("p (b g) -> p b g", b=2)
                        r8 = rr[:, 0:8].rearrange("p (b g) -> p b g", b=2)
                        t8 = rr[:, 8:16].rearrange("p (b g) -> p b g", b=2)
                        S.op("dve", lambda e, nd=nd, a8=a8, t8=t8: e.tensor_tensor(out=t8, in0=nd[:, :, :, 64], in1=a8, op=ALU.mult),
                             reads=bD[1] + [bvecA[t_]], writes=[brr])
                        S.op("dve", lambda e, rr=rr: e.tensor_scalar(out=rr[:, 16:24], in0=rr[:, 8:16], scalar1=-1.0, scalar2=None,
                                                                     op0=ALU.mult), reads=[brr], writes=[brr])
                        S.op("dve", lambda e, rr=rr: e.tensor_tensor(out=rr[:, 8:16], in0=rr[:, 8:16], in1=rr[:, 16:24], op=ALU.max),
                             reads=[brr], writes=[brr])
                        S.op("dve", lambda e, rr=rr: e.tensor_scalar(out=rr[:, 8:16], in0=rr[:, 8:16], scalar1=1.0, scalar2=None,
                                                                     op0=ALU.max), reads=[brr], writes=[brr])
                        S.op("dve", lambda e, rr=rr: e.reciprocal(rr[:, 8:16], rr[:, 8:16]), reads=[brr], writes=[brr])
                        S.op("dve", lambda e, a8=a8, t8=t8, r8=r8: e.tensor_tensor(out=r8, in0=a8, in1=t8, op=ALU.mult),
                             reads=[brr, bvecA[t_]], writes=[brr])
                        if d_ == 1:
                            for b_ in range(2):
                                S.op("dve", lambda e, b_=b_, nd=nd, rr=rr, t_=t_: e.tensor_tensor(
                                    out=hfA[:, t_, b_ * 256:(b_ + 1) * 256].rearrange("p (g d) -> p g d", d=64),
                                    in0=nd[:, b_, :, 0:64], in1=rr[:, b_ * 4:(b_ + 1) * 4].unsqueeze(2).to_broadcast([128, 4, 64]), op=ALU.mult),
                                    reads=bD[1] + [brr], writes=[bhfA[t_]])
                        else:
                            hs, bhs = hs_r.get(); hq, bhq = h2_r.get(); rs, brs = rs_r.get(); mo, bmo = mo_r.get(); mTt, bmTt = mT_r.get()
                            for b_ in range(2):
                                S.op("dve", lambda e, b_=b_, nd=nd, rr=rr, hs=hs: e.tensor_tensor(
                                    out=hs[:, b_ * 256:(b_ + 1) * 256].rearrange("p (g d) -> p g d", d=64),
                                    in0=nd[:, b_, :, 0:64], in1=rr[:, b_ * 4:(b_ + 1) * 4].unsqueeze(2).to_broadcast([128, 4, 64]), op=ALU.mult),
                                    reads=bD[1] + [brr], writes=[bhs])
                            S.op("pool", lambda e, hs=hs, t_=t_: e.tensor_tensor(out=hs[:], in0=hs[:], in1=hfA[:, t_, :], op=ALU.add),
                                 reads=[bhs, bhfA[t_]], writes=[bhs])
                            S.op("pool", lambda e, hs=hs, hq=hq: e.tensor_tensor(out=hq[:], in0=hs[:], in1=hs[:], op=ALU.mult),
                                 reads=[bhs], writes=[bhq])
                            S.op("dve", lambda e, hq=hq, rs=rs: e.tensor_reduce(out=rs[:], in_=hq[:].rearrange("p (h d) -> p h d", d=64),
                                                                                 axis=AX.X, op=ALU.add), reads=[bhq], writes=[brs])
                            rsqrt_mean(rs[:], rs[:], HD, [brs], [brs])
                            S.op("pool", lambda e, hs=hs, rs=rs: e.tensor_tensor(
                                out=hs[:].rearrange("p (h d) -> p h d", d=64), in0=hs[:].rearrange("p (h d) -> p h d", d=64),
                                in1=rs[:].unsqueeze(2).to_broadcast([128, 8, 64]), op=ALU.mult), reads=[bhs, brs], writes=[bhs])
                            S.op("pool", lambda e, hs=hs: e.tensor_tensor(out=hs[:], in0=hs[:], in1=mhw[:], op=ALU.mult),
                                 reads=[bhs, b_mhw], writes=[bhs])
                            S.op("dve", lambda e, hs=hs, mo=mo, t_=t_: e.tensor_tensor(out=mo[:], in0=hs[:], in1=sgA[:, t_, :], op=ALU.mult),
                                 reads=[bhs, bsgA], writes=[bmo])
                            pT_, bpT_ = next_pT()
                            S.group("pe", [lambda e, j=j, mo=mo, pT_=pT_: e.transpose(pT_[:, j * 128:(j + 1) * 128], mo[:, j * 128:(j + 1) * 128], ident[:])
                                           for j in range(4)], reads=[bmo, b_ident], writes=[bpT_])
                            S.op("act", lambda e, pT_=pT_, mTt=mTt: e.activation(out=mTt[:].rearrange("p a b -> p (a b)"), in_=pT_[:, 0:512], func=AF.Copy),
                                 reads=[bpT_], writes=[bmTt])
                            S.dma(mixT_d[0:512, tok0 + t_ * 128:tok0 + (t_ + 1) * 128].rearrange("(j p) t -> p j t", p=128), mTt[:], bmTt,
                                  reads=[bmTt], writes=[b_mixT])
                    cbs = {-1: cbh["cur"]}
                    no = len(order)
                    for i in range(no + 1):
                        recs = []
                        if i < no:
                            recs.append(S.record(lambda: cbs.__setitem__(i, scan_S(order[i]))))
                        if i >= 1:
                            c_ = cbs.pop(i - 2)
                            recs.append(S.record(lambda: scan_H(order[i - 1], c_)))
                        S.interleave(recs)
                S.flush()

            with ExitStack() as st:
                pT = [(ps(st, f"pT{i}", [128, 1024], BF16), Buf(f"pT{i}", True)) for i in range(2)]
                Dp = [ps(st, f"D{i}", [128, 1024], F32) for i in range(3)]
                bD = [[Buf(f"D{i}lo", True), Buf(f"D{i}hi", True)] for i in range(3)]
                tc = TileCtx(st, pT)
                wom = sb(st, "wom", [128, 4, D], BF16); bwom = Buf("wom")
                woa = sb(st, "woa", [64, 8, D], BF16); bwoa = Buf("woa")
                S.dma(wom[:], w_out_s[0:512, :].rearrange("(k p) c -> p k c", p=128), bwom, reads=[b_wouts], writes=[bwom])
                S.dma(woa[:], w_out_s[512:1024, :].rearrange("(k p) c -> p k c", p=64), bwoa, reads=[b_wouts], writes=[bwoa])
                mm_r = Ring(nc, st, "mm", [128, 4, 512], BF16, 2)
                ma_r = Ring(nc, st, "ma", [64, 8, 512], BF16, 2)
                x1_r = Ring(nc, st, "x1", [128, D], F32, 2)
                mixg = {}

                def out_tile(t_):
                    g4 = t_ // 4
                    if g4 not in mixg:
                        n4 = min(4, NT - g4 * 4)
                        mmg, bmmg = mm_r.get(); mag, bmag = ma_r.get()
                        cs4 = slice(tok0 + g4 * 512, tok0 + g4 * 512 + n4 * 128)
                        S.dma(mmg[:, :, 0:n4 * 128], mixT_d[0:512, cs4].rearrange("(j p) t -> p j t", p=128), bmmg, reads=[b_mixT], writes=[bmmg])
                        S.dma(mag[:, :, 0:n4 * 128], mixT_d[512:1024, cs4].rearrange("(j p) t -> p j t", p=64), bmag, reads=[b_mixT], writes=[bmag])
                        mixg.clear()
                        mixg[g4] = (mmg, bmmg, mag, bmag)
                    mmg, bmm, mag, bma = mixg[g4]
                    o4 = (t_ % 4) * 128
                    mm = mmg[:, :, o4:o4 + 128]
                    ma = mag[:, :, o4:o4 + 128]
                    x1, bx1 = x1_r.get()
                    xt, bx = tc.xt.get()
                    S.dma(xt[:], xs_d[base + t_], bx, writes=[bx])
                    dq = t_ % 2
                    for cb in range(2):
                        fns = [lambda e, k=k, cb=cb, mm=mm: e.matmul(Dp[dq][:, cb * 512:(cb + 1) * 512], lhsT=mm[:, k, :],
                                                                     rhs=wom[:, k, cb * 512:(cb + 1) * 512], start=(k == 0), stop=False) for k in range(4)]
                        fns += [lambda e, k=k, cb=cb, ma=ma: e.matmul(Dp[dq][:, cb * 512:(cb + 1) * 512], lhsT=ma[:, k, :],
                                                                      rhs=woa[:, k, cb * 512:(cb + 1) * 512], start=False, stop=(k == 7)) for k in range(8)]
                        S.group("pe", fns, reads=[bmm, bma, bwom, bwoa], writes=[bD[dq][cb]])
                    S.op("dve", lambda e, x1=x1, xt=xt, dq=dq: e.tensor_tensor(out=x1[:], in0=Dp[dq][:], in1=xt[:], op=ALU.add),
                         reads=bD[dq] + [bx], writes=[bx1])
                    if t_ < M:
                        gi = main_base[si] + t_
                        S.dma(x1_d[gi], x1[:], bx1, reads=[bx1], writes=[b_x1d[gi]])
                    return (x1, bx1)

                def out_tile_B(t_, hx):
                    x1, bx1 = hx
                    h2T, bh2T = tc.norm_T_from_sbuf(x1, bx1)
                    if t_ < M:
                        S.dma(h2T_d[si][:, 1 + t_ * 128:1 + (t_ + 1) * 128].rearrange("(k p) t -> p k t", p=128), h2T[:], bh2T,
                              reads=[bh2T], writes=[b_h2d[si]])
                        if not halo and t_ == 0:
                            S.dma(h2T_d[si][:, 0:1].rearrange("(k p) t -> p k t", p=128), h2T[:, :, 0:1], bh2T, reads=[bh2T], writes=[b_h2d[si]])
                        if not halo and t_ == M - 1:
                            S.dma(h2T_d[si][:, M * 128 + 1:M * 128 + 2].rearrange("(k p) t -> p k t", p=128), h2T[:, :, 127:128], bh2T,
                                  reads=[bh2T], writes=[b_h2d[si]])
                    elif t_ == M:
                        S.dma(h2T_d[si][:, 0:1].rearrange("(k p) t -> p k t", p=128), h2T[:, :, 127:128], bh2T, reads=[bh2T], writes=[b_h2d[si]])
                    else:
                        S.dma(h2T_d[si][:, M * 128 + 1:M * 128 + 2].rearrange("(k p) t -> p k t", p=128), h2T[:, :, 0:1], bh2T,
                              reads=[bh2T], writes=[b_h2d[si]])
                hx = {}
                for i in range(NT + 1):
                    recs = []
                    if i < NT:
                        recs.append(S.record(lambda: hx.__setitem__(i, out_tile(i))))
                    if i >= 1:
                        h_ = hx.pop(i - 1)
                        recs.append(S.record(lambda: out_tile_B(i - 1, h_)))
                    S.interleave(recs)
                S.flush()

        with ExitStack() as st:
            Dp = [ps(st, f"D{i}", [128, 1024], F32) for i in range(4)]
            bD = [[Buf(f"D{i}lo", True), Buf(f"D{i}hi", True)] for i in range(4)]
            wup = sb(st, "wup", [128, 8, 2 * DFF], BF16); bwup = Buf("wup")
            wdn = sb(st, "wdn", [128, NPAIR, D], BF16); bwdn = Buf("wdn")
            for k in range(8):
                S.dma(wup[:, k, :], w_up_s[k * 128:(k + 1) * 128, :], bwup, reads=[b_wups], writes=[bwup])
            S.dma(wdn[:], w_dn_s.rearrange("(j p) c -> p j c", p=128), bwdn, reads=[b_wdns], writes=[bwdn])
            cw = sb(st, "cw", [128, NFC, 4], F32); bcw = Buf("cw")
            S.dma(cw[:].rearrange("p a b -> p (a b)"), cw_d, bcw, writes=[bcw])
            GW = cfg.GT * 128
            h2w_r = Ring(nc, st, "h2win", [128, 8, GW + 2], BF16, 1)
            uh_r = Ring(nc, st, "uh", [128, NFC, 2], F32, 1)
            fg_r = Ring(nc, st, "fg", [128, 2], F32, 2)
            U_r = Ring(nc, st, "Ub", [128, GW + 2], F32, 4)
            acc_r = Ring(nc, st, "acc", [128, GW], F32, 5)
            actT = sb(st, "actT", [128, NPAIR, GW], BF16); bact = Buf("actT")
            x1_r = Ring(nc, st, "fx1", [128, D], F32, 2)
            sq_r = Ring(nc, st, "fsq", [128, D], BF16, 1)
            ss_r = Ring(nc, st, "fss", [128, 2], F32, 2)
            pic = [0]
            for si, (sname, M, halo) in enumerate(cfg.segs):
                ngr = (M + cfg.GT - 1) // cfg.GT
                def ffn_group(g, si=si, M=M, halo=halo, ngr=ngr):
                    t0 = g * cfg.GT
                    ntl = min(cfg.GT, M - t0)
                    gw = ntl * 128
                    h2win, bh2w = h2w_r.get(); uh, buh = uh_r.get(); fg, bfg = fg_r.get()
                    S.dma(h2win[:, :, 0:gw + 2], h2T_d[si][:, t0 * 128:t0 * 128 + gw + 2].rearrange("(k p) t -> p k t", p=128), bh2w,
                          reads=[b_h2d[si]], writes=[bh2w])
                    S.op("pool", lambda e, fg=fg: e.memset(fg[:], 1.0 if halo else 0.0), writes=[bfg])
                    if halo and g == 0:
                        S.op("pool", lambda e, fg=fg: e.tensor_copy(fg[:, 0:1], flo[:, 0:1]), reads=[b_flo, bfg], writes=[bfg])
                    if halo and g == ngr - 1:
                        S.op("pool", lambda e, fg=fg: e.tensor_copy(fg[:, 1:2], flo[:, 1:2]), reads=[b_flo, bfg], writes=[bfg])
                    if not halo:
                        if g > 0:
                            S.op("pool", lambda e, fg=fg: e.memset(fg[:, 0:1], 1.0), reads=[bfg], writes=[bfg])
                        if g < ngr - 1:
                            S.op("pool", lambda e, fg=fg: e.memset(fg[:, 1:2], 1.0), reads=[bfg], writes=[bfg])
                    fns = []
                    for c in range(NFC):
                        for k in range(8):
                            fns.append(lambda e, c=c, k=k, h2win=h2win, gw=gw: e.matmul(Dp[3][:, 2 * c:2 * c + 2], lhsT=wup[:, k, c * 128:(c + 1) * 128],
                                                                                       rhs=h2win[:, k, 0:gw + 2:gw + 1], start=(k == 0), stop=(k == 7)))
                    S.group("pe", fns, reads=[bh2w, bwup], writes=[bD[3][0]])
                    S.op("dve", lambda e, uh=uh, fg=fg: e.tensor_tensor(out=uh[:], in0=Dp[3][:, 0:2 * NFC].rearrange("p (c t) -> p c t", t=2),
                                                                        in1=fg[:].unsqueeze(1).to_broadcast([128, NFC, 2]), op=ALU.mult),
                         reads=[bD[3][0], bfg], writes=[buh])
                    def stageA(j):
                        outs = []
                        for half, c in enumerate((j, NPAIR + j)):
                            pi = pic[0]
                            pic[0] += 1
                            pu = Dp[pi % 3][:, (pi // 3 % 2) * 512:(pi // 3 % 2) * 512 + 512]
                            bpu = bD[pi % 3][pi // 3 % 2]
                            U, bU = U_r.get()
                            S.group("pe", [lambda e, c=c, k=k, pu=pu: e.matmul(pu[:, 0:gw], lhsT=wup[:, k, c * 128:(c + 1) * 128],
                                                                              rhs=h2win[:, k, 1:gw + 1], start=(k == 0), stop=(k == 7))
                                           for k in range(8)], reads=[bh2w, bwup], writes=[bpu])
                            acc, bacc = acc_r.get()
                            S.op("act", lambda e, U=U, pu=pu: e.activation(out=U[:, 1:gw + 1], in_=pu[:, 0:gw], func=AF.Copy), reads=[bpu], writes=[bU])
                            S.op("act", lambda e, acc=acc, pu=pu, c=c: e.activation(out=acc[:, 0:gw], in_=pu[:, 0:gw], func=AF.Identity,
                                                                                    scale=cw[:, c, 1:2], bias=cw[:, c, 3:4]),
                                 reads=[bpu, bcw], writes=[bacc])
                            S.op("pool", lambda e, U=U, c=c: e.tensor_copy(U[:, 0:gw + 2:gw + 1], uh[:, c, :]), reads=[buh, bU], writes=[bU])
                            outs.append((U, bU, c, acc, bacc))
                        return outs

                    def stageB(j, outs):
                        accs = []
                        for half, (U, bU, c, acc, bacc) in enumerate(outs):
                            S.op("dve", lambda e, U=U, acc=acc, c=c: e.scalar_tensor_tensor(out=acc[:, 0:gw], in0=U[:, 0:gw], scalar=cw[:, c, 0:1],
                                                                                           in1=acc[:, 0:gw], op0=ALU.mult, op1=ALU.add),
                                 reads=[bU, bcw, bacc], writes=[bacc])
                            S.op("dve", lambda e, U=U, acc=acc, c=c: e.scalar_tensor_tensor(out=acc[:, 0:gw], in0=U[:, 2:gw + 2], scalar=cw[:, c, 2:3],
                                                                                           in1=acc[:, 0:gw], op0=ALU.mult, op1=ALU.add),
                                 reads=[bU, bcw, bacc], writes=[bacc])
                            accs.append((acc, bacc))
                        (aa, baa), (ag, bag) = accs
                        S.op("act", lambda e: e.activation(out=ag[:, 0:gw], in_=ag[:, 0:gw], func=AF.Silu), reads=[bag], writes=[bag])
                        S.op("pool", lambda e: e.tensor_tensor(out=actT[:, j, 0:gw], in0=aa[:, 0:gw], in1=ag[:, 0:gw], op=ALU.mult),
                             reads=[baa, bag], writes=[bact])

                    pend = {}
                    for j in range(NPAIR + 1):
                        recs = []
                        if j < NPAIR:
                            recs.append(S.record(lambda: pend.__setitem__(j, stageA(j))))
                        if j >= 1:
                            o_ = pend.pop(j - 1)
                            recs.append(S.record(lambda: stageB(j - 1, o_)))
                        S.interleave(recs)
                    for tl in range(ntl):
                        gi = main_base[si] + t0 + tl
                        x1, bx1 = x1_r.get(); sq, bsq = sq_r.get(); ss, bss = ss_r.get()
                        y, by = x1, bx1
                        S.dma(x1[:], x1_d[gi], bx1, reads=[b_x1d[gi]], writes=[bx1])
                        dq = tl % 2
                        for cb in range(2):
                            S.group("pe", [lambda e, j=j, cb=cb, tl=tl, dq=dq: e.matmul(Dp[dq][:, cb * 512:(cb + 1) * 512], lhsT=actT[:, j, tl * 128:(tl + 1) * 128],
                                                                                       rhs=wdn[:, j, cb * 512:(cb + 1) * 512], start=(j == 0), stop=(j == NPAIR - 1))
                                           for j in range(NPAIR)], reads=[bact, bwdn], writes=[bD[dq][cb]])
                        S.op("dve", lambda e, y=y, x1=x1, dq=dq: e.tensor_tensor(out=y[:], in0=Dp[dq][:], in1=x1[:], op=ALU.add),
                             reads=bD[dq] + [bx1], writes=[bx1])
                        S.op("act", lambda e, sq=sq, y=y, ss=ss: e.activation(out=sq[:], in_=y[:], func=AF.Square, accum_out=ss[:, 0:1]),
                             reads=[by], writes=[bsq, bss])
                        rsqrt_mean(ss[:, 1:2], ss[:, 0:1], D, [bss], [bss])
                        S.op("dve", lambda e, y=y, ss=ss: e.scalar_tensor_tensor(out=y[:], in0=y[:], scalar=ss[:, 1:2], in1=wfin[:], op0=ALU.mult, op1=ALU.mult),
                             reads=[by, bss, b_wfin], writes=[by])
                        S.dma(y_d[gi], y[:], by, reads=[by], writes=[b_y])
                for g in range(ngr):
                    ffn_group(g)
            S.flush()
    return nc


def rope_tables(pos):
    pos = np.asarray(pos)
    row = (pos // 64).astype(np.float32)
    col = (pos % 64).astype(np.float32)
    nf = HD // 4
    inv = (np.float32(10000.0) ** (-np.arange(nf, dtype=np.float32) / np.float32(nf))).astype(np.float32)
    ar = row[:, None] * inv
    ac = col[:, None] * inv
    ang = np.concatenate([ar, ar, ac, ac], axis=-1).astype(np.float32)
    cos = np.cos(ang).astype(np.float32)
    sin = np.sin(ang).astype(np.float32)
    sgn = np.concatenate([-np.ones(16), np.ones(16), -np.ones(16), np.ones(16)]).astype(np.float32)
    return np.concatenate([cos, sin * sgn], axis=-1).astype(np.float32)


def prep_inputs(cfg, x_prompt, x_sample, w_in, b_gates, mh_norm_w, q_norm_w, k_norm_w, w_out, norm1_w, norm2_w,
                w_up, conv_w, conv_b, w_down, final_norm_w):
    f32 = np.float32
    xp = np.asarray(x_prompt, f32).reshape(-1, 128, D)
    xsm = np.asarray(x_sample, f32).reshape(x_sample.shape[0], -1, 128, D)
    NP, SEG, NCTX = cfg.NP, cfg.SEG, cfg.NCTX
    hperm = [0, 4, 1, 5, 2, 6, 3, 7]
    w_in0 = np.asarray(w_in, f32)[0]
    aq = w_in0[:, C_AQ:C_AQ + 512].reshape(D, 8, 64)[:, hperm, :].reshape(D, 512)
    w_in_p = np.concatenate([w_in0[:, :C_AQ], aq, w_in0[:, C_AQ + 512:]], axis=1)
    w_out0 = np.asarray(w_out, f32)[0]
    wa = w_out0[512:].reshape(8, 64, D)[hperm].reshape(512, D)
    w_out_p = np.concatenate([w_out0[:512], wa], axis=0)
    rep = lambda v: np.ascontiguousarray(np.broadcast_to(np.asarray(v, f32).reshape(1, -1), (128, np.asarray(v).size)))
    pk = lambda v: np.ascontiguousarray(np.asarray(v, f32).reshape(8, 128).T)
    cwv = np.asarray(conv_w, f32)[0]
    cbv = np.asarray(conv_b, f32)[0]
    convpk = np.stack([cwv[0], cwv[1], cwv[2], cbv], axis=-1).reshape(NFC, 128, 4).transpose(1, 0, 2).reshape(128, NFC * 4)
    wqk = np.concatenate([np.tile(np.asarray(q_norm_w, f32)[0], 8), np.tile(np.asarray(k_norm_w, f32)[0], 2)])
    common = {
        "w_in": np.ascontiguousarray(w_in_p), "w_out": np.ascontiguousarray(w_out_p),
        "w_up": np.ascontiguousarray(np.asarray(w_up, f32)[0]), "w_down": np.ascontiguousarray(np.asarray(w_down, f32)[0]),
        "n1pk": pk(np.asarray(norm1_w)[0]), "n2pk": pk(np.asarray(norm2_w)[0]), "convpk": np.ascontiguousarray(convpk),
        "bg_bc": rep(np.asarray(b_gates)[0]), "mhw_bc": rep(np.asarray(mh_norm_w)[0]), "wqk_bc": rep(wqk),
        "wfin_bc": rep(final_norm_w),
    }
    samp_pos = rope_tables(np.arange(cfg.ST * 128)).reshape(cfg.ST, 128, 128)
    zt = np.zeros((128, D), f32)
    in_maps = []
    for c in range(cfg.NC):
        tiles, cs = [], []
        for i in range(cfg.NSAMP):
            b = c * cfg.NSAMP + i
            tiles += [xsm[b, t] for t in range(cfg.ST)]
            cs += [samp_pos[t] for t in range(cfg.ST)]
        lo, hi = c * SEG, (c + 1) * SEG
        ptiles = list(range(lo, hi)) + [lo - 1, hi]
        for t in ptiles:
            ok = 0 <= t < NP
            tiles.append(xp[t] if ok else zt)
            cs.append(rope_tables(np.arange(t * 128, (t + 1) * 128) if ok else np.zeros(128, np.int64)))
        left = list(range(0, max(lo - 1, 0)))
        right = list(range(NP - 1, hi, -1))
        ctx = [(t, 1.0, 0.0) for t in left] + [(t, 0.0, 1.0) for t in right]
        while len(ctx) < max(NCTX, 1):
            ctx.append((None, 0.0, 0.0))
        xc = np.stack([xp[t] if t is not None else zt for t, _, _ in ctx])
        csc = np.stack([rope_tables(np.arange(t * 128, (t + 1) * 128) if t is not None else np.zeros(128, np.int64)) for t, _, _ in ctx])
        fl = np.array([[l, r, 0.0 if t is None else 1.0] for t, l, r in ctx], f32).reshape(1, -1)
        m = dict(common)
        m.update({
            "xs": np.ascontiguousarray(np.stack(tiles)), "xc": np.ascontiguousarray(xc),
            "cs_own": np.ascontiguousarray(np.stack(cs)), "cs_ctx": np.ascontiguousarray(csc),
            "fl_ctx": np.ascontiguousarray(np.broadcast_to(fl, (128, fl.shape[1]))),
            "fl_own": np.ascontiguousarray(np.broadcast_to(np.array([[float(lo > 0), float(hi < NP), 0, 0]], f32), (128, 4))),
        })
        in_maps.append(m)
    return in_maps


def assemble(cfg, results, nb_samp):
    yp = np.zeros((cfg.NP, 128, D), np.float32)
    ys = np.zeros((nb_samp, cfg.ST, 128, D), np.float32)
    for c in range(cfg.NC):
        y = np.asarray(results[c]["y"])
        o = 0
        for i in range(cfg.NSAMP):
            ys[c * cfg.NSAMP + i] = y[o:o + cfg.ST]
            o += cfg.ST
        yp[c * cfg.SEG:(c + 1) * cfg.SEG] = y[o:o + cfg.SEG]
    return yp.reshape(1, cfg.NP * 128, D), ys.reshape(nb_samp, cfg.ST * 128, D)


_CACHE = {}


def run(cfg, **inputs):
    key = (cfg.NC, cfg.SEG, cfg.ST, cfg.NSAMP, cfg.GT)
    if key not in _CACHE:
        _CACHE[key] = build_program(cfg)
    nc = _CACHE[key]
    in_maps = prep_inputs(cfg, **inputs)
    res = run_bass_kernel_spmd(nc, in_maps, core_ids=list(range(cfg.NC)))
    return assemble(cfg, res.results, inputs["x_sample"].shape[0])


def kernel(**inputs):
    cfg = Cfg()
    return run(cfg, **inputs)
```
